# Optimizing a Trainium2 kernel written in Bass

```python
import jax, jax.numpy as jnp
from jax import lax
import numpy as np

D_MODEL = 2048
BATCH = 4
SEQ = 4096
DEPTH = 2

CHUNK = 64
Q_BLOCK = 128
N_A = DEPTH // 2
N_B = DEPTH - N_A

RET_HEADS = 8
RET_QK = D_MODEL
RET_V = 2 * D_MODEL
RET_DK = RET_QK // RET_HEADS
RET_DV = RET_V // RET_HEADS

MLA_HEADS = 16
MLA_NOPE = 128
MLA_ROPE = 64
MLA_V = 128
Q_RANK = 512
KV_RANK = 512

D_FF = 5632
ROPE_THETA = 10000.0
EPS = 1e-6
N_MOD = 9
ADA_SCALE = 0.5

kernel_name = "yoco_retention_mla_macaron_adaln"


def rmsnorm(x, g):
    x32 = x.astype(jnp.float32)
    y = x32 * lax.rsqrt(jnp.mean(x32 * x32, axis=-1, keepdims=True) + EPS)
    return y.astype(x.dtype) * g


def modulate(x, g, shift, scale):
    return rmsnorm(x, g) * (1.0 + scale[:, None, :]) + shift[:, None, :]


def rope(x, pos):
    half = x.shape[-1] // 2
    inv = ROPE_THETA ** (-jnp.arange(half, dtype=jnp.float32) / half)
    ang = pos.astype(jnp.float32)[..., None] * inv
    cos = jnp.cos(ang)[:, :, None, :].astype(x.dtype)
    sin = jnp.sin(ang)[:, :, None, :].astype(x.dtype)
    x1, x2 = x[..., :half], x[..., half:]
    return jnp.concatenate([x1 * cos - x2 * sin, x1 * sin + x2 * cos], axis=-1)


def swiglu(h, w_in, w_out):
    gate, up = jnp.split(h @ w_in, 2, axis=-1)
    return (jax.nn.silu(gate) * up) @ w_out


def chunk_retention(q, k, v):
    B, S, H, dk = q.shape
    dv = v.shape[-1]
    nc = S // CHUNK
    dt = q.dtype
    log_g = jnp.log1p(-(2.0 ** (-5.0 - jnp.arange(H, dtype=jnp.float32))))
    idx = jnp.arange(CHUNK, dtype=jnp.float32)
    d_intra = jnp.exp(log_g[:, None, None] * jnp.abs(idx[:, None] - idx[None, :])).astype(dt)
    xi = jnp.exp(log_g[:, None] * (idx + 1.0)).astype(dt)
    zeta = jnp.exp(log_g[:, None] * (CHUNK - 1.0 - idx)).astype(dt)
    g_chunk = jnp.exp(log_g * CHUNK).astype(dt)

    def to_chunks(t):
        return t.reshape(B, nc, CHUNK, H, t.shape[-1]).transpose(1, 0, 3, 2, 4)

    def step(state, inp):
        qc, kc, vc = inp
        s = jnp.einsum('bhid,bhjd->bhij', qc, kc) * d_intra
        o = (jnp.einsum('bhij,bhjv->bhiv', s, vc)
             + jnp.einsum('bhid,bhdv->bhiv', qc * xi[None, :, :, None], state))
        state = (state * g_chunk[None, :, None, None]
                 + jnp.einsum('bhjd,bhjv->bhdv', kc * zeta[None, :, :, None], vc))
        return state, o

    state0 = jnp.zeros((B, H, dk, dv), dt)
    _, o = lax.scan(step, state0, (to_chunks(q), to_chunks(k), to_chunks(v)))
    return o.transpose(1, 0, 3, 2, 4).reshape(B, S, H, dv)


def retention_mixer(h, pos, w_in, gn_g, w_out):
    B, S, _ = h.shape
    q, k, v, g = jnp.split(h @ w_in, [RET_QK, 2 * RET_QK, 2 * RET_QK + RET_V], axis=-1)
    q = rope(q.reshape(B, S, RET_HEADS, RET_DK), pos) * (RET_DK ** -0.5)
    k = rope(k.reshape(B, S, RET_HEADS, RET_DK), pos)
    v = v.reshape(B, S, RET_HEADS, RET_DV)
    o = chunk_retention(q, k, v).astype(jnp.float32)
    mu = jnp.mean(o, axis=-1, keepdims=True)
    var = jnp.mean(jnp.square(o - mu), axis=-1, keepdims=True)
    on = ((o - mu) * lax.rsqrt(var + EPS)).reshape(B, S, RET_V).astype(h.dtype) * gn_g
    return (jax.nn.silu(g) * on) @ w_out


def shared_kv(x, c, pos, kv_ada_w, kv_ada_b, kv_norm_g, w_dkv, kv_latent_g, w_ukv):
    B, S, _ = x.shape
    shift, scale = jnp.split(jax.nn.silu(c) @ kv_ada_w + kv_ada_b, 2, axis=-1)
    hk = modulate(x, kv_norm_g, shift, scale)
    ckv, kr = jnp.split(hk @ w_dkv, [KV_RANK], axis=-1)
    ckv = rmsnorm(ckv, kv_latent_g)
    kr = rope(kr[:, :, None, :], pos)[:, :, 0, :]
    kv = (ckv @ w_ukv).reshape(B, S, MLA_HEADS, MLA_NOPE + MLA_V)
    kn, v = jnp.split(kv, [MLA_NOPE], axis=-1)
    return kn, kr, v


def mla_mixer(h, pos, kn, kr, v, w_dq, q_latent_g, w_uq, w_out):
    B, S, _ = h.shape
    cq = rmsnorm(h @ w_dq, q_latent_g)
    q = (cq @ w_uq).reshape(B, S, MLA_HEADS, MLA_NOPE + MLA_ROPE)
    qn, qr = jnp.split(q, [MLA_NOPE], axis=-1)
    qr = rope(qr, pos)
    scale = (MLA_NOPE + MLA_ROPE) ** -0.5
    nb = S // Q_BLOCK
    qn_b = qn.reshape(B, nb, Q_BLOCK, MLA_HEADS, MLA_NOPE).transpose(1, 0, 2, 3, 4)
    qr_b = qr.reshape(B, nb, Q_BLOCK, MLA_HEADS, MLA_ROPE).transpose(1, 0, 2, 3, 4)
    k_chunk = jnp.arange(S) // CHUNK

    def one_block(args):
        qnb, qrb, bi = args
        s = (jnp.einsum('bqhd,bkhd->bhqk', qnb, kn)
             + jnp.einsum('bqhr,bkr->bhqk', qrb, kr)).astype(jnp.float32) * scale
        q_chunk = (bi * Q_BLOCK + jnp.arange(Q_BLOCK)) // CHUNK
        mask = k_chunk[None, :] <= q_chunk[:, None]
        s = jnp.where(mask[None, None], s, -1e30)
        p = jax.nn.softmax(s, axis=-1).astype(v.dtype)
        return jnp.einsum('bhqk,bkhd->bqhd', p, v)

    o = lax.map(one_block, (qn_b, qr_b, jnp.arange(nb)))
    o = o.transpose(1, 0, 2, 3, 4).reshape(B, S, MLA_HEADS * MLA_V)
    return o @ w_out


def setup_inputs(seed: int = 0) -> dict:
    key = jax.random.key(seed)
    ks = jax.random.split(key, 24)
    f32 = jnp.float32
    D = D_MODEL

    def nrm(k, shape, fan_in, mult=1.0):
        return jax.random.normal(k, shape, f32) * (mult * fan_in ** -0.5)

    def gain(k, shape):
        return 1.0 + 0.02 * jax.random.normal(k, shape, f32)

    x = jax.random.normal(ks[0], (BATCH, SEQ, D), f32)
    c = jax.random.normal(ks[1], (BATCH, D), f32)
    offset = jax.random.randint(ks[2], (BATCH, 1), 0, 1024, dtype=jnp.int32)
    positions = offset + jnp.arange(SEQ, dtype=jnp.int32)[None, :]
    return {
        "x": x,
        "c": c,
        "positions": positions,
        "ada_w": nrm(ks[3], (DEPTH, D, N_MOD * D), D, ADA_SCALE),
        "ada_b": 0.02 * jax.random.normal(ks[4], (DEPTH, N_MOD * D), f32),
        "norm_g": gain(ks[5], (DEPTH, 3, D)),
        "ffn_w_in": nrm(ks[6], (DEPTH, 2, D, 2 * D_FF), D),
        "ffn_w_out": nrm(ks[7], (DEPTH, 2, D_FF, D), D_FF),
        "ret_w_in": nrm(ks[8], (N_A, D, 2 * RET_QK + 2 * RET_V), D),
        "ret_gn_g": gain(ks[9], (N_A, RET_V)),
        "ret_w_out": nrm(ks[10], (N_A, RET_V, D), RET_V),
        "kv_ada_w": nrm(ks[11], (D, 2 * D), D, ADA_SCALE),
        "kv_ada_b": 0.02 * jax.random.normal(ks[12], (2 * D,), f32),
        "kv_norm_g": gain(ks[13], (D,)),
        "mla_w_dkv": nrm(ks[14], (D, KV_RANK + MLA_ROPE), D),
        "kv_latent_g": gain(ks[15], (KV_RANK,)),
        "mla_w_ukv": nrm(ks[16], (KV_RANK, MLA_HEADS * (MLA_NOPE + MLA_V)), KV_RANK),
        "mla_w_dq": nrm(ks[17], (N_B, D, Q_RANK), D),
        "q_latent_g": gain(ks[18], (N_B, Q_RANK)),
        "mla_w_uq": nrm(ks[19], (N_B, Q_RANK, MLA_HEADS * (MLA_NOPE + MLA_ROPE)), Q_RANK),
        "mla_w_out": nrm(ks[20], (N_B, MLA_HEADS * MLA_V, D), MLA_HEADS * MLA_V),
        "final_g": gain(ks[21], (D,)),
    }


def reference(x, c, positions, ada_w, ada_b, norm_g, ffn_w_in, ffn_w_out,
              ret_w_in, ret_gn_g, ret_w_out, kv_ada_w, kv_ada_b, kv_norm_g,
              mla_w_dkv, kv_latent_g, mla_w_ukv, mla_w_dq, q_latent_g, mla_w_uq,
              mla_w_out, final_g):
    c_act = jax.nn.silu(c)
    kn = kr = v = None
    for l in range(DEPTH):
        mods = jnp.split(c_act @ ada_w[l] + ada_b[l], N_MOD, axis=-1)
        sh1, sc1, gt1, shm, scm, gtm, sh2, sc2, gt2 = mods
        if l == N_A:
            kn, kr, v = shared_kv(x, c, positions, kv_ada_w, kv_ada_b, kv_norm_g,
                                  mla_w_dkv, kv_latent_g, mla_w_ukv)
        h = modulate(x, norm_g[l, 0], sh1, sc1)
        x = x + 0.5 * gt1[:, None, :] * swiglu(h, ffn_w_in[l, 0], ffn_w_out[l, 0])
        h = modulate(x, norm_g[l, 1], shm, scm)
        if l < N_A:
            y = retention_mixer(h, positions, ret_w_in[l], ret_gn_g[l], ret_w_out[l])
        else:
            j = l - N_A
            y = mla_mixer(h, positions, kn, kr, v, mla_w_dq[j], q_latent_g[j],
                          mla_w_uq[j], mla_w_out[j])
        x = x + gtm[:, None, :] * y
        h = modulate(x, norm_g[l, 2], sh2, sc2)
        x = x + 0.5 * gt2[:, None, :] * swiglu(h, ffn_w_in[l, 1], ffn_w_out[l, 1])
    return rmsnorm(x, final_g)
```

```python
import contextlib
import math

import ml_dtypes
import numpy as np

import concourse.bass as bass
import concourse.mybir as mybir
from concourse.bass_utils import run_bass_kernel_spmd

F32 = mybir.dt.float32
BF16 = mybir.dt.bfloat16
I32 = mybir.dt.int32
AF = mybir.ActivationFunctionType
ALU = mybir.AluOpType
AX = mybir.AxisListType

D = 2048
SEQ = 4096
TOK = 2048
NBLK = TOK // 128
KC = D // 128
DFF = 5632
EPS = 1e-6
RET_H = 8
MLA_H = 16
PI = math.pi
PAIRS = [[0, 1], [2, 3], [4, 5], [6, 7]]
NCORES = 8


class S:
    def __init__(self, h):
        self.h = h
        self.count = 0


class Buf:
    __slots__ = ("w", "r")

    def __init__(self):
        self.w = None
        self.r = {}


class EngRec:
    def __init__(self, name, sem):
        self.name = name
        self.sem = sem
        self.ops = []
        self.waited = {}


class Prog:
    ENG = ("pe", "act", "dve", "pool", "sp")

    def __init__(self, nc, es, nsem=56):
        self.nc = nc
        self.pool = [es.enter_context(nc.semaphore(f"s{i}")) for i in range(nsem)]
        self.eng = {n: EngRec(n, S(self.pool.pop())) for n in self.ENG}
        self.dsems = []
        self.free = []

    def dsem(self):
        if self.free:
            return self.free.pop()
        s = S(self.pool.pop())
        self.dsems.append(s)
        return s

    def release(self, sems):
        self.free.extend(sems)

    def op(self, eng, fn, reads=(), writes=(), sig=True, dsem=None, inc=16):
        e = self.eng[eng]
        need = {}

        def add(s, v):
            if eng == "pe" and s is e.sem:
                return
            if need.get(s, 0) < v:
                need[s] = v

        for b in reads:
            if b.w is not None:
                add(*b.w)
        for b in writes:
            if b.w is not None:
                add(*b.w)
            for s, v in b.r.items():
                add(s, v)
        waits = []
        for s, v in need.items():
            if e.waited.get(s, 0) >= v:
                continue
            e.waited[s] = v
            waits.append((s, v))
        if dsem is not None:
            dsem.count += (1 if inc is None else inc)
            ev = (dsem, dsem.count)
            sig_t = (dsem, inc)
        elif sig:
            e.sem.count += 1
            ev = (e.sem, e.sem.count)
            sig_t = (e.sem, 1)
        else:
            ev = (e.sem, e.sem.count + 1)
            sig_t = None
        for b in reads:
            if b.r.get(ev[0], 0) < ev[1]:
                b.r[ev[0]] = ev[1]
        for b in writes:
            b.w = ev
            b.r = {}
        e.ops.append((waits, fn, sig_t))
        return ev

    def barrier(self):
        for n in self.ENG:
            e = self.eng[n]
            waits = []
            for m in self.ENG:
                s = self.eng[m].sem
                if m != n and s.count > e.waited.get(s, 0):
                    e.waited[s] = s.count
                    waits.append((s, s.count))
            for s in self.dsems:
                if s.count > e.waited.get(s, 0):
                    e.waited[s] = s.count
                    waits.append((s, s.count))
            if waits:
                e.ops.append((waits, None, None))

    def emit(self):
        nc = self.nc
        self.nemit = getattr(self, "nemit", 0) + 1
        with nc.named_scope(f"{getattr(self, 'phase', 'ph')}_{self.nemit}"), nc.Block() as block:
            def mk(name):
                e = self.eng[name]

                def body(engobj):
                    for waits, fn, sig_t in e.ops:
                        for s, v in waits:
                            engobj.wait_ge(s.h, v)
                        if fn is None:
                            continue
                        ins = fn(engobj)
                        if sig_t is not None:
                            if sig_t[1] is None:
                                ins.then_inc(sig_t[0].h)
                            else:
                                ins.then_inc(sig_t[0].h, sig_t[1])
                    e.ops = []
                return body
            reg = {"pe": block.tensor, "act": block.scalar, "dve": block.vector,
                   "pool": block.gpsimd, "sp": block.sync}
            for n in self.ENG:
                if self.eng[n].ops:
                    reg[n](mk(n))


class Slot:
    def __init__(self, t, sem=None):
        self.t = t
        self.b = Buf()
        self.sem = sem


class Ring:
    def __init__(self, slots):
        self.slots = slots
        self.i = 0

    def next(self):
        s = self.slots[self.i % len(self.slots)]
        self.i += 1
        return s


class K:
    def __init__(self, stage=99, debug=False):
        self.stage = stage
        self.debug = debug
        self.nc = bass.Bass("TRN2", target_bir_lowering=False)
        self.es = contextlib.ExitStack()
        self.uid = 0

    def din(self, name, shape, dt=F32):
        return self.nc.dram_tensor(name, list(shape), dt, kind="ExternalInput").ap()

    def dscr(self, name, shape, dt=F32):
        return self.nc.dram_tensor(name, list(shape), dt, kind="Internal").ap()

    def sb(self, es, name, shape, dt=F32):
        self.uid += 1
        return es.enter_context(self.nc.sbuf_tensor(f"{name}_u{self.uid}", list(shape), dt))

    def ps(self, es, name, shape, dt=F32):
        self.uid += 1
        return es.enter_context(self.nc.psum_tensor(f"{name}_u{self.uid}", list(shape), dt))

    def ring(self, es, name, n, shape, dt=F32, dma=False, psum=False):
        mk = self.ps if psum else self.sb
        self.uid += 1
        slots = [Slot(mk(es, f"{name}_{self.uid}_{i}", shape, dt), self.P.dsem() if dma else None)
                 for i in range(n)]
        if dma:
            es.callback(self.P.release, [sl.sem for sl in slots])
        return Ring(slots)

    def slot(self, es, name, shape, dt=F32, dma=False, psum=False):
        return self.ring(es, name, 1, shape, dt, dma, psum).slots[0]

    def dma(self, q, out, in_, reads, writes, sem):
        self.P.op(q, lambda e: e.dma_start(out=out, in_=in_), reads=reads, writes=writes, dsem=sem)

    def build(self):
        nc, es = self.nc, self.es
        with es:
            self.P = P = Prog(nc, es)
            self.declare_io()
            self.setup_consts()
            self.phase_mods()
            x_src, xb_src = self.x_in, self.xin_b
            if self.stage >= 1:
                self.ffn(0, 0, site=0, grow=0, x_src=self.x_in, xsb=self.xin_b)
            if self.stage >= 2:
                self.retention()
            if self.stage >= 3:
                self.ffn(0, 1, site=2, grow=2, x_src=self.xs, xsb=self.xs_b)
            if self.stage >= 4:
                self.shared_kv()
            if self.stage >= 5:
                self.ffn(1, 0, site=4, grow=3, x_src=self.xs, xsb=self.xs_b)
            if self.stage >= 6:
                self.mla()
            if self.stage >= 7:
                self.ffn(1, 1, site=6, grow=5, x_src=self.xs, xsb=self.xs_b)
            self.final_out()
        return nc

    def declare_io(self):
        self.x_in = self.din("x_in", [TOK, D])
        self.c_col = self.din("c_col", [128, KC])
        self.pos = self.din("pos", [1, TOK], I32)
        self.flag = self.din("flag", [128, 2])
        self.mods_w = self.din("mods_w", [D, 20480])
        self.mods_b = self.din("mods_b", [1, 20480])
        self.normg_col = self.din("normg_col", [128, 7, KC])
        self.ffn_w_in = self.din("ffn_w_in", [2, 2, D, 2 * DFF])
        self.ffn_w_out = self.din("ffn_w_out", [2, 2, DFF, D])
        self.ret_w_in = self.din("ret_w_in", [1, D, 12288])
        self.ret_gn_g = self.din("ret_gn_g", [1, 4096])
        self.ret_w_out = self.din("ret_w_out", [1, 4096, D])
        self.w_dkv = self.din("mla_w_dkv", [D, 576])
        self.kvlat_col = self.din("kvlat_col", [128, 4])
        self.w_ukv = self.din("mla_w_ukv", [512, 4096])
        self.w_dq = self.din("mla_w_dq", [1, D, 512])
        self.qlat_col = self.din("qlat_col", [128, 4])
        self.w_uq = self.din("mla_w_uq", [1, 512, 3072])
        self.mla_w_out = self.din("mla_w_out", [1, D, D])
        self.final_g = self.din("final_g", [1, D])
        self.c_ident = self.din("c_ident", [128, 128], BF16)
        self.c_invf = self.din("c_invf", [128, 4])
        self.c_dT = self.din("c_dT", [128, RET_H, 128])
        self.c_zeta = self.din("c_zeta", [128, RET_H])
        self.c_xi = self.din("c_xi", [128, RET_H, 128])
        self.c_diag = self.din("c_diag", [128, 128])
        self.c_mlamask = self.din("c_mlamask", [128, 4, 512])
        self.out = self.nc.dram_tensor("out", [TOK, D], F32, kind="ExternalOutput").ap()
        self.xs = self.dscr("xs", [TOK, D])
        self.gates = self.dscr("gates", [6, D])
        self.xin_b = [Buf() for _ in range(NBLK)]
        self.xs_b = [Buf() for _ in range(NBLK)]
        self.gates_b = Buf()
        if self.debug:
            self.dbg = self.nc.dram_tensor("dbg", [128, 4096], F32, kind="ExternalOutput").ap()

    def setup_consts(self):
        nc, P, es = self.nc, self.P, self.es
        self.ident = Slot(self.sb(es, "ident", [128, 128], BF16), P.dsem())
        self.modcol = Slot(self.sb(es, "modcol", [128, 14, KC]))
        self.Acol = Slot(self.sb(es, "Acol", [128, 7, KC]))
        self.Bcol = Slot(self.sb(es, "Bcol", [128, 7, KC]))
        self.flag_sb = Slot(self.sb(es, "flag_sb", [128, 2]), P.dsem())
        self.dma("sp", self.ident.t[:], self.c_ident, [], [self.ident.b], self.ident.sem)
        self.dma("sp", self.flag_sb.t[:], self.flag, [], [self.flag_sb.b], self.flag_sb.sem)

    def phase_mods(self):
        self.P.phase = 'mods'
        nc, P = self.nc, self.P
        NCH = 40
        msrc = self.dscr("msrc", [1, NCH * 512])
        mdst = self.dscr("mdst", [2, NCH * 512])
        with contextlib.ExitStack() as es:
            ccol = self.slot(es, "ccol", [128, KC], F32, dma=True)
            ngc = self.slot(es, "ngc", [128, 7, KC], F32, dma=True)
            cact = self.slot(es, "cact", [128, KC], BF16)
            one = self.slot(es, "one", [1, 1])
            loc = self.slot(es, "mloc", [1, NCH * 512], F32, dma=True)
            wring = self.ring(es, "mw", 2, [128, KC, 512], BF16, dma=True)
            bring = self.ring(es, "mb", 2, [1, 512], F32, dma=True)
            rring = self.ring(es, "mr", 4, [1, 512], F32, dma=True)
            gsem = self.ring(es, "mgs", 4, [1, 2], F32, dma=True)
            pring = self.ring(es, "mp", 2, [128, 512], F32, psum=True)
            cring = self.ring(es, "mc", 2, [128, 4], F32, psum=True)
            csem = P.dsem()
            self.dma("sp", ccol.t[:], self.c_col, [], [ccol.b], ccol.sem)
            self.dma("sp", ngc.t[:], self.normg_col, [], [ngc.b], ngc.sem)
            P.op("act", lambda e: e.activation(out=cact.t[:], in_=ccol.t[:], func=AF.Silu),
                 reads=[ccol.b], writes=[cact.b])
            P.op("dve", lambda e: e.memset(one.t[:], 1.0), writes=[one.b])
            wv = self.mods_w.rearrange("(kc p) n -> p kc n", p=128)
            for cc in range(NCH):
                wt = wring.next()
                self.dma("pool", wt.t[:], wv[:, :, cc * 512:(cc + 1) * 512], [], [wt.b], wt.sem)
                bt = bring.next()
                self.dma("sp", bt.t[:], self.mods_b[:, cc * 512:(cc + 1) * 512], [], [bt.b], bt.sem)
                pt = pring.next()
                for kc in range(KC):
                    P.op("pe", lambda e, kc=kc, pt=pt, wt=wt: e.matmul(
                        pt.t[0:1, :], cact.t[:, kc:kc + 1], wt.t[:, kc, :],
                        start=(kc == 0), stop=(kc == KC - 1)),
                        reads=[cact.b, wt.b], writes=[pt.b], sig=(kc == KC - 1))
                P.op("dve", lambda e, pt=pt, bt=bt, cc=cc: e.tensor_tensor(
                    out=loc.t[:, cc * 512:(cc + 1) * 512], in0=pt.t[0:1, :], in1=bt.t[:], op=ALU.add),
                    reads=[pt.b, bt.b], writes=[loc.b])
            sb_, db_ = Buf(), Buf()
            self.dma("sp", msrc, loc.t[:], [loc.b], [sb_], loc.sem)
            P.op("pool", lambda e: e.collective_compute(
                "AllGather", ALU.bypass, replica_groups=PAIRS, ins=[msrc], outs=[mdst]),
                reads=[sb_], writes=[db_], dsem=csem, inc=None)
            kinds_all = []
            for l in range(2):
                base = 6 * l + (2 if l == 1 else 0)
                kinds = [("col", base + 0), ("col", base + 1), ("gate", 3 * l + 0),
                         ("col", base + 2), ("col", base + 3), ("gate", 3 * l + 1),
                         ("col", base + 4), ("col", base + 5), ("gate", 3 * l + 2)]
                if l == 1:
                    kinds_all += [("col", 6), ("col", 7)]
                kinds_all += kinds
            gc = 0
            for (kind, idx) in kinds_all:
                for cc in range(4):
                    r, lc = divmod(gc, NCH)
                    gc += 1
                    srow = mdst[r:r + 1, lc * 512:(lc + 1) * 512]
                    if kind == "gate":
                        gs = gsem.next()
                        self.dma("sp", self.gates[idx:idx + 1, cc * 512:(cc + 1) * 512], srow,
                                 [db_], [self.gates_b], gs.sem)
                    else:
                        rt = rring.next()
                        self.dma("sp", rt.t[:], srow, [db_], [rt.b], rt.sem)
                        ct = cring.next()
                        for j in range(4):
                            P.op("pe", lambda e, j=j, ct=ct, rt=rt: e.matmul(
                                ct.t[:, j:j + 1], rt.t[0:1, j * 128:(j + 1) * 128], one.t[:],
                                start=True, stop=True),
                                reads=[rt.b, one.b], writes=[ct.b], sig=(j == 3))
                        P.op("dve", lambda e, ct=ct, idx=idx, cc=cc: e.tensor_copy(
                            out=self.modcol.t[:, idx, cc * 4:(cc + 1) * 4], in_=ct.t[:]),
                            reads=[ct.b], writes=[self.modcol.b])
            es.callback(P.release, [csem])
            for site in range(7):
                sh, sc = 2 * site, 2 * site + 1
                P.op("dve", lambda e, site=site, sc=sc: e.scalar_tensor_tensor(
                    out=self.Acol.t[:, site, :], in0=self.modcol.t[:, sc, :], scalar=1.0,
                    in1=ngc.t[:, site, :], op0=ALU.add, op1=ALU.mult),
                    reads=[self.modcol.b, ngc.b], writes=[self.Acol.b])
                P.op("dve", lambda e, site=site, sh=sh: e.tensor_copy(
                    out=self.Bcol.t[:, site, :], in_=self.modcol.t[:, sh, :]),
                    reads=[self.modcol.b], writes=[self.Bcol.b])
            if self.debug and self.stage == 0:
                o = Slot(self.sb(es, "dbgo", [128, 4096]), P.dsem())
                P.op("dve", lambda e: e.memset(o.t[:], 0.0), writes=[o.b])
                P.op("dve", lambda e: e.tensor_copy(out=o.t[:, 0:7 * KC], in_=self.Acol.t[:].rearrange("p a b -> p (a b)")),
                     reads=[self.Acol.b], writes=[o.b])
                P.op("dve", lambda e: e.tensor_copy(out=o.t[:, 112:112 + 7 * KC], in_=self.Bcol.t[:].rearrange("p a b -> p (a b)")),
                     reads=[self.Bcol.b], writes=[o.b])
                self.dma("sp", self.dbg, o.t[:], [o.b], [Buf()], o.sem)
            P.barrier()
            P.emit()

    def prologue(self, es, site, x_src, xsb, hT, ptr_slots=None, nxin=3):
        nc, P = self.nc, self.P
        xin = self.ring(es, "pxin", nxin, [128, D], F32, dma=True)
        junk = Slot(self.sb(es, "pjunk", [128, D], BF16))
        xn = self.ring(es, "pxn", 2 if ptr_slots is not None else 3, [128, D], BF16)
        st = self.ring(es, "pst", 4, [128, 4], F32)
        ptr = Ring(ptr_slots) if ptr_slots is not None else self.ring(es, "ptr", 4, [128, 4, 128], BF16, psum=True)
        xbs = {}

        def stage_a(blk):
            xt = xin.next()
            self.dma("sp", xt.t[:], x_src[blk * 128:(blk + 1) * 128, :], [xsb[blk]], [xt.b], xt.sem)
            s = st.next()
            P.op("act", lambda e, xt=xt, s=s: e.activation(
                out=junk.t[:], in_=xt.t[:], func=AF.Square, accum_out=s.t[:, 0:1]),
                reads=[xt.b], writes=[junk.b, s.b])
            P.op("act", lambda e, s=s: e.activation(
                out=s.t[:, 1:2], in_=s.t[:, 0:1], func=AF.Ln, scale=1.0 / D, bias=EPS),
                reads=[s.b], writes=[s.b])
            P.op("act", lambda e, s=s: e.activation(
                out=s.t[:, 2:3], in_=s.t[:, 1:2], func=AF.Exp, scale=-0.5),
                reads=[s.b], writes=[s.b])
            xb = xn.next()
            P.op("dve", lambda e, xb=xb, xt=xt, s=s: e.tensor_scalar(
                out=xb.t[:], in0=xt.t[:], scalar1=s.t[:, 2:3], scalar2=None, op0=ALU.mult),
                reads=[xt.b, s.b], writes=[xb.b])
            xbs[blk] = xb

        def stage_b(blk):
            xb = xbs.pop(blk)
            for g in range(4):
                pt = ptr.next()
                for j in range(4):
                    kc = g * 4 + j
                    P.op("pe", lambda e, pt=pt, j=j, kc=kc, xb=xb: e.transpose(
                        pt.t[:, j, :], xb.t[:, kc * 128:(kc + 1) * 128], self.ident.t[:]),
                        reads=[xb.b, self.ident.b], writes=[pt.b], sig=(j == 3))
                for j in range(4):
                    kc = g * 4 + j
                    dst = hT.t[:, kc, blk * 128:(blk + 1) * 128]
                    if j % 2 == 0:
                        P.op("act", lambda e, pt=pt, j=j, kc=kc, dst=dst: e.activation(
                            out=dst, in_=pt.t[:, j, :], func=AF.Identity,
                            scale=self.Acol.t[:, site, kc:kc + 1], bias=self.Bcol.t[:, site, kc:kc + 1]),
                            reads=[pt.b, self.Acol.b, self.Bcol.b], writes=[hT.b])
                    else:
                        P.op("dve", lambda e, pt=pt, j=j, kc=kc, dst=dst: e.tensor_scalar(
                            out=dst, in0=pt.t[:, j, :], scalar1=self.Acol.t[:, site, kc:kc + 1],
                            scalar2=self.Bcol.t[:, site, kc:kc + 1], op0=ALU.mult, op1=ALU.add),
                            reads=[pt.b, self.Acol.b, self.Bcol.b], writes=[hT.b])

        stage_a(0)
        for blk in range(NBLK):
            if blk + 1 < NBLK:
                stage_a(blk + 1)
            stage_b(blk)

    def outproj_rings(self, es, nch):
        return (self.ring(es, "wo", 2, [128, nch, 256], BF16, dma=True),
                self.ring(es, "xc", 7, [128, 256], F32, dma=True),
                self.ring(es, "tm", 2, [128, 256], F32),
                self.ring(es, "py", 2, [128, 256], F32, psum=True))

    def outproj(self, es, actT, nch, tok0, nblk, w_rows, gate_bc, gscale, x_src, xsb_src, first, rings=None):
        nc, P = self.nc, self.P
        wo, xc, tm, py = rings if rings is not None else self.outproj_rings(es, nch)
        src, sbufs = (x_src, xsb_src) if first else (self.xs, self.xs_b)
        chunks = [(dc, bi) for dc in range(8) for bi in range(nblk)]
        LA = 5
        xts = {}

        def issue_load(i):
            dc, bi = chunks[i]
            blk = tok0 // 128 + bi
            xt = xc.next()
            xts[i] = xt
            self.dma("sp", xt.t[:], src[blk * 128:(blk + 1) * 128, dc * 256:(dc + 1) * 256],
                     [sbufs[blk]], [xt.b], xt.sem)

        for i in range(min(LA, len(chunks))):
            issue_load(i)
        wt = None
        for i, (dc, bi) in enumerate(chunks):
            blk = tok0 // 128 + bi
            if bi == 0:
                wt = wo.next()
                self.dma("pool", wt.t[:, 0:nch, :], w_rows(dc).rearrange("(c p) n -> p c n", p=128), [], [wt.b], wt.sem)
            if i + LA < len(chunks):
                issue_load(i + LA)
            xt = xts.pop(i)
            pt = py.next()
            for c in range(nch):
                P.op("pe", lambda e, pt=pt, c=c, bi=bi, wt=wt: e.matmul(
                    pt.t[:], actT.t[:, c, bi * 128:(bi + 1) * 128], wt.t[:, c, :],
                    start=(c == 0), stop=(c == nch - 1)),
                    reads=[actT.b, wt.b], writes=[pt.b], sig=(c == nch - 1))
            t = tm.next()
            P.op("dve", lambda e, t=t, pt=pt, dc=dc: e.scalar_tensor_tensor(
                out=t.t[:], in0=pt.t[:], scalar=gscale, in1=gate_bc.t[:, dc * 256:(dc + 1) * 256],
                op0=ALU.mult, op1=ALU.mult),
                reads=[pt.b, gate_bc.b], writes=[t.b])
            P.op("dve", lambda e, t=t, xt=xt: e.tensor_tensor(
                out=xt.t[:], in0=xt.t[:], in1=t.t[:], op=ALU.add),
                reads=[t.b, xt.b], writes=[xt.b])
            self.dma("sp", self.xs[blk * 128:(blk + 1) * 128, dc * 256:(dc + 1) * 256], xt.t[:],
                     [xt.b], [self.xs_b[blk]], xt.sem)

    def ffn(self, l, j, site, grow, x_src, xsb):
        nc, P = self.nc, self.P
        P.phase = f'ffn{l}{j}'
        w_in = self.ffn_w_in[l, j].rearrange("(kc p) n -> p kc n", p=128)
        w_out = self.ffn_w_out[l, j]
        with contextlib.ExitStack() as es:
            hT = Slot(self.sb(es, "hT", [128, KC, TOK], BF16))
            gate = Slot(self.sb(es, "gate", [128, D]), P.dsem())
            self.dma("sp", gate.t[:], self.gates[grow:grow + 1, :].partition_broadcast(128),
                     [self.gates_b], [gate.b], gate.sem)
            hid = Slot(self.sb(es, "hid", [128, 12, TOK], BF16))
            wg = self.ring(es, "wg", 2, [128, KC, 256], BF16, dma=True)
            wu = self.ring(es, "wu", 2, [128, KC, 256], BF16, dma=True)
            pg = self.ring(es, "pg", 2, [128, 512], F32, psum=True)
            pu = self.ring(es, "pu", 2, [128, 512], F32, psum=True)
            sg = self.ring(es, "sg", 2, [128, 512], F32)
            orings = self.outproj_rings(es, 12)
            class _V:
                pass
            pviews = []
            for sl in pg.slots + pu.slots:
                v_ = _V()
                v_.t = sl.t[:].bitcast(BF16).rearrange("p (a b) -> p a b", b=128)
                v_.b = sl.b
                v_.sem = None
                pviews.append(v_)
            with contextlib.ExitStack() as es2:
                self.prologue(es2, site, x_src, xsb, hT, ptr_slots=pviews, nxin=2)
            groups = [(0, 6), (6, 5), (11, 6), (17, 5)]
            for gi, (t0, nt) in enumerate(groups):
                for ti in range(nt):
                    t = t0 + ti
                    g_t, u_t = wg.next(), wu.next()
                    self.dma("pool", g_t.t[:], w_in[:, :, t * 256:(t + 1) * 256], [], [g_t.b], g_t.sem)
                    self.dma("pool", u_t.t[:], w_in[:, :, DFF + t * 256:DFF + (t + 1) * 256], [], [u_t.b], u_t.sem)
                    for sub in range(2):
                        c = ti * 2 + sub
                        for tt in range(4):
                            pgt, put = pg.next(), pu.next()
                            for (wt, pt) in ((g_t, pgt), (u_t, put)):
                                for kc in range(KC):
                                    P.op("pe", lambda e, wt=wt, pt=pt, kc=kc, sub=sub, tt=tt: e.matmul(
                                        pt.t[:], wt.t[:, kc, sub * 128:(sub + 1) * 128],
                                        hT.t[:, kc, tt * 512:(tt + 1) * 512],
                                        start=(kc == 0), stop=(kc == KC - 1)),
                                        reads=[wt.b, hT.b], writes=[pt.b], sig=(kc == KC - 1))
                            s = sg.next()
                            P.op("act", lambda e, s=s, pgt=pgt: e.activation(
                                out=s.t[:], in_=pgt.t[:], func=AF.Silu), reads=[pgt.b], writes=[s.b])
                            P.op("dve", lambda e, s=s, put=put, c=c, tt=tt: e.tensor_tensor(
                                out=hid.t[:, c, tt * 512:(tt + 1) * 512], in0=s.t[:], in1=put.t[:],
                                op=ALU.mult), reads=[s.b, put.b], writes=[hid.b])
                nch = nt * 2
                r0 = t0 * 256
                self.outproj(es, hid, nch, 0, NBLK,
                             lambda dc, r0=r0, nch=nch: w_out[r0:r0 + nch * 128, dc * 256:(dc + 1) * 256],
                             gate, 0.5, x_src, xsb, first=(gi == 0), rings=orings)
            P.barrier()
            P.emit()


    def trig(self, es, npart, invcol, phases, names):
        P = self.P
        tabs = [self.slot(es, nm, [npart, TOK]) for nm in names]
        invf = self.slot(es, "invf", [128, 4], dma=True)
        self.dma("sp", invf.t[:], self.c_invf, [], [invf.b], invf.sem)
        TWO_PI = 2.0 * PI
        with contextlib.ExitStack() as es2:
            posi = self.slot(es2, "posi", [npart, TOK], I32, dma=True)
            posf = self.slot(es2, "posf", [npart, TOK])
            u = self.slot(es2, "tu", [npart, TOK])
            ki = self.slot(es2, "tki", [npart, TOK], I32)
            kf = self.slot(es2, "tkf", [npart, TOK])
            m = self.slot(es2, "tm_", [npart, TOK])
            self.dma("sp", posi.t[:], self.pos.partition_broadcast(npart), [], [posi.b], posi.sem)
            P.op("dve", lambda e: e.tensor_copy(out=posf.t[:], in_=posi.t[:]), reads=[posi.b], writes=[posf.b])
            for tab, ph in zip(tabs, phases):
                P.op("dve", lambda e, ph=ph: e.tensor_scalar(
                    out=u.t[:], in0=posf.t[:], scalar1=invf.t[0:npart, invcol:invcol + 1],
                    scalar2=(ph if isinstance(ph, float) else invf.t[0:npart, ph:ph + 1]),
                    op0=ALU.mult, op1=ALU.add), reads=[posf.b, invf.b], writes=[u.b])
                P.op("dve", lambda e: e.tensor_scalar(
                    out=ki.t[:], in0=u.t[:], scalar1=1.0 / TWO_PI, scalar2=None, op0=ALU.mult),
                    reads=[u.b], writes=[ki.b])
                P.op("dve", lambda e: e.tensor_copy(out=kf.t[:], in_=ki.t[:]), reads=[ki.b], writes=[kf.b])
                P.op("dve", lambda e: e.scalar_tensor_tensor(
                    out=u.t[:], in0=kf.t[:], scalar=-TWO_PI, in1=u.t[:], op0=ALU.mult, op1=ALU.add),
                    reads=[kf.b, u.b], writes=[u.b])
                P.op("dve", lambda e: e.tensor_scalar(
                    out=m.t[:], in0=u.t[:], scalar1=PI, scalar2=TWO_PI, op0=ALU.is_gt, op1=ALU.mult),
                    reads=[u.b], writes=[m.b])
                P.op("dve", lambda e: e.tensor_tensor(out=u.t[:], in0=u.t[:], in1=m.t[:], op=ALU.subtract),
                     reads=[u.b, m.b], writes=[u.b])
                P.op("dve", lambda e: e.tensor_scalar(
                    out=m.t[:], in0=u.t[:], scalar1=-PI, scalar2=TWO_PI, op0=ALU.is_lt, op1=ALU.mult),
                    reads=[u.b], writes=[m.b])
                P.op("dve", lambda e: e.tensor_tensor(out=u.t[:], in0=u.t[:], in1=m.t[:], op=ALU.add),
                     reads=[u.b, m.b], writes=[u.b])
                P.op("dve", lambda e: e.tensor_scalar(
                    out=u.t[:], in0=u.t[:], scalar1=-3.141592, scalar2=3.141592, op0=ALU.max, op1=ALU.min),
                    reads=[u.b], writes=[u.b])
                P.op("act", lambda e, tab=tab: e.activation(out=tab.t[:], in_=u.t[:], func=AF.Sin),
                     reads=[u.b], writes=[tab.b])
            P.barrier()
            P.emit()
        return tabs

    def rope_fm(self, p0, p1, cos, sin, tsl, out0, out1, outb, rt):
        P = self.P
        t1, t2, t3, t4 = rt.next(), rt.next(), rt.next(), rt.next()
        for (t, p, tab) in ((t1, p0, cos), (t2, p1, sin), (t3, p0, sin), (t4, p1, cos)):
            P.op("dve", lambda e, t=t, p=p, tab=tab: e.tensor_tensor(
                out=t.t[:], in0=p.t[:], in1=tab.t[:, tsl], op=ALU.mult),
                reads=[p.b, tab.b], writes=[t.b])
        P.op("dve", lambda e: e.tensor_tensor(out=out0, in0=t1.t[:], in1=t2.t[:], op=ALU.subtract),
             reads=[t1.b, t2.b], writes=[outb])
        P.op("dve", lambda e: e.tensor_tensor(out=out1, in0=t3.t[:], in1=t4.t[:], op=ALU.add),
             reads=[t3.b, t4.b], writes=[outb])

    def retention(self):
        nc, P = self.nc, self.P
        P.phase = 'ret'
        w_in = self.ret_w_in[0].rearrange("(kc p) n -> p kc n", p=128)
        g128 = _consts()["_g128"]
        gsc = self.dscr("gsc", [TOK, 4096], BF16)
        gsc_b = [Buf() for _ in range(NBLK)]
        xsrc = [self.dscr(f"rxs{h}", [128, 1024]) for h in range(RET_H)]
        xdst = [self.dscr(f"rxd{h}", [256, 1024]) for h in range(RET_H)]
        csem = P.dsem()
        with contextlib.ExitStack() as es:
            hT = self.slot(es, "hT", [128, KC, TOK], BF16)
            with contextlib.ExitStack() as es2:
                self.prologue(es2, 1, self.xs, self.xs_b, hT)
                P.barrier()
                P.emit()
            sinT, cosT = self.trig(es, 128, 0, [0.0, PI / 2], ["sinT", "cosT"])
            qT = self.slot(es, "qT", [128, 2, TOK], BF16)
            qxT = self.slot(es, "qxT", [128, 2, TOK], BF16)
            kT = self.slot(es, "kT", [128, 2, TOK], BF16)
            kz = self.slot(es, "kz", [128, NBLK, 2, 128], BF16)
            v = self.slot(es, "v", [128, NBLK, 512], BF16)
            gg = self.slot(es, "gg", [128, NBLK, 512], BF16)
            wr = self.ring(es, "rw", 3, [128, KC, 256], BF16, dma=True)
            S_f = self.slot(es, "S_f", [128, 2, 512])
            S_br = self.ring(es, "S_b", 2, [128, 2, 512], BF16)
            Sst_sem = P.dsem()
            Sin_ = self.slot(es, "Sin", [128, 2, 512], F32, dma=True)
            gnb = self.slot(es, "gnb", [128, 512], F32, dma=True)
            dT = self.slot(es, "dT", [128, RET_H, 128], F32, dma=True)
            zeta = self.slot(es, "zeta", [128, RET_H], F32, dma=True)
            xi = self.slot(es, "xi", [128, RET_H, 128], F32, dma=True)
            rt = self.ring(es, "rrt", 4, [128, 512], F32)
            sTm = self.ring(es, "sTm", 3, [128, 128], BF16)
            onr = self.ring(es, "onr", 2, [128, 512], F32)
            stt = self.ring(es, "stt", 2, [128, 12], F32)
            gout = self.ring(es, "gout", 2, [128, 512], BF16, dma=True)
            pp = self.ring(es, "rpp", 6, [128, 512], F32, psum=True)
            ptr = self.ring(es, "rpt", 2, [128, 4, 128], BF16, psum=True)
            self.dma("sp", dT.t[:], self.c_dT, [], [dT.b], dT.sem)
            self.dma("sp", zeta.t[:], self.c_zeta, [], [zeta.b], zeta.sem)
            self.dma("sp", xi.t[:], self.c_xi, [], [xi.b], xi.sem)

            def proj_fm(wt, outT, h):
                for tt in range(4):
                    p0, p1 = pp.next(), pp.next()
                    for dch, p in ((0, p0), (1, p1)):
                        for kc in range(KC):
                            P.op("pe", lambda e, p=p, kc=kc, dch=dch, tt=tt: e.matmul(
                                p.t[:], wt.t[:, kc, dch * 128:(dch + 1) * 128],
                                hT.t[:, kc, tt * 512:(tt + 1) * 512],
                                start=(kc == 0), stop=(kc == KC - 1)),
                                reads=[wt.b, hT.b], writes=[p.b], sig=(kc == KC - 1))
                    tsl = slice(tt * 512, (tt + 1) * 512)
                    self.rope_fm(p0, p1, cosT, sinT, tsl, outT.t[:, 0, tsl], outT.t[:, 1, tsl], outT.b, rt)

            def proj_tm(w0, w1, evac):
                for blk in range(NBLK):
                    p = pp.next()
                    for half, wt in enumerate((w0, w1)):
                        for kc in range(KC):
                            P.op("pe", lambda e, p=p, kc=kc, half=half, wt=wt, blk=blk: e.matmul(
                                p.t[:, half * 256:(half + 1) * 256], hT.t[:, kc, blk * 128:(blk + 1) * 128],
                                wt.t[:, kc, :], start=(kc == 0), stop=(kc == KC - 1)),
                                reads=[wt.b, hT.b], writes=[p.b], sig=(kc == KC - 1))
                    evac(blk, p)

            def state_step(blk, h):
                pS = [pp.next(), pp.next()]
                for dch in range(2):
                    P.op("pe", lambda e, dch=dch, pS=pS, blk=blk: e.matmul(
                        pS[dch].t[:], kz.t[:, blk, dch, :], v.t[:, blk, :], start=True, stop=True),
                        reads=[kz.b, v.b], writes=[pS[dch].b])
                for dch in range(2):
                    P.op("dve", lambda e, dch=dch, pS=pS, h=h: e.scalar_tensor_tensor(
                        out=S_f.t[:, dch, :], in0=S_f.t[:, dch, :], scalar=g128[h], in1=pS[dch].t[:],
                        op0=ALU.mult, op1=ALU.add), reads=[S_f.b, pS[dch].b], writes=[S_f.b])

            for h in range(RET_H):
                wk = wr.next()
                self.dma("pool", wk.t[:], w_in[:, :, 2048 + h * 256:2048 + (h + 1) * 256], [], [wk.b], wk.sem)
                wv0, wv1 = wr.next(), wr.next()
                self.dma("pool", wv0.t[:], w_in[:, :, 4096 + h * 512:4096 + h * 512 + 256], [], [wv0.b], wv0.sem)
                self.dma("pool", wv1.t[:], w_in[:, :, 4096 + h * 512 + 256:4096 + (h + 1) * 512], [], [wv1.b], wv1.sem)
                proj_fm(wk, kT, h)
                for blk in range(NBLK):
                    pt = ptr.next()
                    for dch in range(2):
                        P.op("pe", lambda e, pt=pt, dch=dch, blk=blk: e.transpose(
                            pt.t[:, dch, :], kT.t[:, dch, blk * 128:(blk + 1) * 128], self.ident.t[:]),
                            reads=[kT.b, self.ident.b], writes=[pt.b], sig=(dch == 1))
                    P.op("act", lambda e, pt=pt, blk=blk, h=h: e.activation(
                        out=kz.t[:, blk, :, :], in_=pt.t[:, 0:2, :], func=AF.Copy, scale=zeta.t[:, h:h + 1]),
                        reads=[pt.b, zeta.b], writes=[kz.b])
                P.op("dve", lambda e: e.memset(S_f.t[:], 0.0), writes=[S_f.b])

                def evac_v(blk, p, h=h):
                    P.op("act", lambda e, blk=blk, p=p: e.copy(out=v.t[:, blk, :], in_=p.t[:]),
                         reads=[p.b], writes=[v.b])
                    state_step(blk, h)
                proj_tm(wv0, wv1, evac_v)
                xs_b_, xd_b_ = Buf(), Buf()
                self.dma("sp", xsrc[h].rearrange("p (a n) -> p a n", a=2), S_f.t[:], [S_f.b], [xs_b_], Sst_sem)
                P.op("pool", lambda e, h=h: e.collective_compute(
                    "AllGather", ALU.bypass, replica_groups=PAIRS, ins=[xsrc[h]], outs=[xdst[h]]),
                    reads=[xs_b_], writes=[xd_b_], dsem=csem, inc=None)
                self.dma("sp", Sin_.t[:], xdst[h][0:128, :].rearrange("p (a n) -> p a n", a=2),
                         [xd_b_], [Sin_.b], Sin_.sem)
                wq = wr.next()
                self.dma("pool", wq.t[:], w_in[:, :, h * 256:(h + 1) * 256], [], [wq.b], wq.sem)
                wg0, wg1 = wr.next(), wr.next()
                self.dma("pool", wg0.t[:], w_in[:, :, 8192 + h * 512:8192 + h * 512 + 256], [], [wg0.b], wg0.sem)
                self.dma("pool", wg1.t[:], w_in[:, :, 8192 + h * 512 + 256:8192 + (h + 1) * 512], [], [wg1.b], wg1.sem)
                self.dma("sp", gnb.t[:], self.ret_gn_g[0:1, h * 512:(h + 1) * 512].partition_broadcast(128),
                         [], [gnb.b], gnb.sem)
                proj_fm(wq, qT, h)
                for dch in range(2):
                    P.op("dve", lambda e, dch=dch, h=h: e.tensor_tensor(
                        out=qxT.t[:, dch, :].rearrange("p (b n) -> p b n", n=128),
                        in0=qT.t[:, dch, :].rearrange("p (b n) -> p b n", n=128),
                        in1=xi.t[:, h:h + 1, :].to_broadcast([128, NBLK, 128]), op=ALU.mult),
                        reads=[qT.b, xi.b], writes=[qxT.b])
                def evac_g(blk, p):
                    t = rt.next()
                    P.op("act", lambda e, t=t, p=p: e.activation(out=t.t[:], in_=p.t[:], func=AF.Silu),
                         reads=[p.b], writes=[t.b])
                    P.op("dve", lambda e, t=t, blk=blk: e.tensor_tensor(
                        out=gg.t[:, blk, :], in0=t.t[:], in1=gnb.t[:], op=ALU.mult),
                        reads=[t.b, gnb.b], writes=[gg.b])
                proj_tm(wg0, wg1, evac_g)
                P.op("dve", lambda e: e.tensor_scalar(
                    out=S_f.t[:], in0=Sin_.t[:], scalar1=self.flag_sb.t[:, 0:1], scalar2=None, op0=ALU.mult),
                    reads=[Sin_.b, self.flag_sb.b], writes=[S_f.b])
                S_b = S_br.next()
                P.op("act", lambda e, S_b=S_b: e.copy(out=S_b.t[:], in_=S_f.t[:]), reads=[S_f.b], writes=[S_b.b])

                def scores(blk, h=h):
                    bs = slice(blk * 128, (blk + 1) * 128)
                    ps = pp.next()
                    for dch in range(2):
                        P.op("pe", lambda e, ps=ps, dch=dch, bs=bs: e.matmul(
                            ps.t[:, 0:128], kT.t[:, dch, bs], qT.t[:, dch, bs],
                            start=(dch == 0), stop=(dch == 1)),
                            reads=[kT.b, qT.b], writes=[ps.b], sig=(dch == 1))
                    sm = sTm.next()
                    P.op("dve", lambda e, sm=sm, ps=ps, h=h: e.tensor_tensor(
                        out=sm.t[:], in0=ps.t[:, 0:128], in1=dT.t[:, h, :], op=ALU.mult),
                        reads=[ps.b, dT.b], writes=[sm.b])
                    return sm

                sm_next = scores(0)
                for blk in range(NBLK):
                    bs = slice(blk * 128, (blk + 1) * 128)
                    sm = sm_next
                    if blk + 1 < NBLK:
                        sm_next = scores(blk + 1)
                    po = pp.next()
                    P.op("pe", lambda e, po=po, sm=sm, blk=blk: e.matmul(
                        po.t[:], sm.t[:], v.t[:, blk, :], start=True, stop=False),
                        reads=[sm.b, v.b], writes=[po.b], sig=False)
                    for dch in range(2):
                        P.op("pe", lambda e, po=po, dch=dch, bs=bs, S_b=S_b: e.matmul(
                            po.t[:], qxT.t[:, dch, bs], S_b.t[:, dch, :], start=False, stop=(dch == 1)),
                            reads=[qxT.b, S_b.b], writes=[po.b], sig=(dch == 1))
                    if blk < NBLK - 1:
                        state_step(blk, h)
                        S_b = S_br.next()
                        P.op("act", lambda e, S_b=S_b: e.copy(out=S_b.t[:], in_=S_f.t[:]),
                             reads=[S_f.b], writes=[S_b.b])
                    st = stt.next()
                    P.op("dve", lambda e, st=st, po=po: e.bn_stats(out=st.t[:, 0:6], in_=po.t[:]),
                         reads=[po.b], writes=[st.b])
                    P.op("dve", lambda e, st=st: e.bn_aggr(out=st.t[:, 6:8], in_=st.t[:, 0:6]),
                         reads=[st.b], writes=[st.b])
                    P.op("act", lambda e, st=st: e.activation(
                        out=st.t[:, 8:9], in_=st.t[:, 7:8], func=AF.Ln, bias=EPS), reads=[st.b], writes=[st.b])
                    P.op("act", lambda e, st=st: e.activation(
                        out=st.t[:, 9:10], in_=st.t[:, 8:9], func=AF.Exp, scale=-0.5), reads=[st.b], writes=[st.b])
                    on = onr.next()
                    P.op("dve", lambda e, on=on, po=po, st=st: e.tensor_scalar(
                        out=on.t[:], in0=po.t[:], scalar1=st.t[:, 6:7], scalar2=st.t[:, 9:10],
                        op0=ALU.subtract, op1=ALU.mult), reads=[po.b, st.b], writes=[on.b])
                    go = gout.next()
                    P.op("dve", lambda e, go=go, on=on, blk=blk: e.tensor_tensor(
                        out=go.t[:], in0=on.t[:], in1=gg.t[:, blk, :], op=ALU.mult),
                        reads=[on.b, gg.b], writes=[go.b])
                    self.dma("sp", gsc[bs, h * 512:(h + 1) * 512], go.t[:], [go.b], [gsc_b[blk]], go.sem)
            P.barrier()
            P.emit()
        P.release([csem, Sst_sem])
        with contextlib.ExitStack() as es:
            gate = self.slot(es, "gate", [128, D], F32, dma=True)
            self.dma("sp", gate.t[:], self.gates[1:2, :].partition_broadcast(128), [self.gates_b], [gate.b], gate.sem)
            self.tm_outproj(es, gsc, gsc_b, 32, lambda dc: self.ret_w_out[0][:, dc * 256:(dc + 1) * 256], gate, 1.0)

    def tm_outproj(self, es, act_dram, act_b, nch, w_rows, gate, gscale):
        P = self.P
        for th in range(2):
            with contextlib.ExitStack() as es2:
                aT = self.slot(es2, "aT", [128, nch, 1024], BF16)
                ain = self.ring(es2, "ain", 2, [128, nch * 128], BF16, dma=True)
                ptr = self.ring(es2, "opt", 2, [128, 4, 128], BF16, psum=True)
                for bi in range(8):
                    blk = th * 8 + bi
                    a = ain.next()
                    self.dma("sp", a.t[:], act_dram[blk * 128:(blk + 1) * 128, :], [act_b[blk]], [a.b], a.sem)
                    for g in range(nch // 4):
                        pt = ptr.next()
                        for j in range(4):
                            c = g * 4 + j
                            P.op("pe", lambda e, pt=pt, j=j, c=c, a=a: e.transpose(
                                pt.t[:, j, :], a.t[:, c * 128:(c + 1) * 128], self.ident.t[:]),
                                reads=[a.b, self.ident.b], writes=[pt.b], sig=(j == 3))
                        dst = aT.t[:, g * 4:(g + 1) * 4, bi * 128:(bi + 1) * 128]
                        if g % 2 == 0:
                            P.op("act", lambda e, pt=pt, dst=dst: e.copy(out=dst, in_=pt.t[:]),
                                 reads=[pt.b], writes=[aT.b])
                        else:
                            P.op("dve", lambda e, pt=pt, dst=dst: e.tensor_copy(out=dst, in_=pt.t[:]),
                                 reads=[pt.b], writes=[aT.b])
                self.outproj(es2, aT, nch, th * 1024, 8, w_rows, gate, gscale, self.xs, self.xs_b, first=False)
                P.barrier()
                P.emit()


    def latent_proj(self, es, hT, wt, gcol, outT, pp, ptr):
        P = self.P
        st = self.ring(es, "lst", 3, [128, 4], F32)
        junk = self.slot(es, "ljunk", [128, 512], BF16)
        cn = self.ring(es, "lcn", 2, [128, 512], BF16)
        for blk in range(NBLK):
            p = pp.next()
            for kc in range(KC):
                P.op("pe", lambda e, p=p, kc=kc, blk=blk: e.matmul(
                    p.t[:], hT.t[:, kc, blk * 128:(blk + 1) * 128], wt.t[:, kc, :],
                    start=(kc == 0), stop=(kc == KC - 1)),
                    reads=[hT.b, wt.b], writes=[p.b], sig=(kc == KC - 1))
            s = st.next()
            P.op("act", lambda e, p=p, s=s: e.activation(
                out=junk.t[:], in_=p.t[:], func=AF.Square, accum_out=s.t[:, 0:1]),
                reads=[p.b], writes=[junk.b, s.b])
            P.op("act", lambda e, s=s: e.activation(
                out=s.t[:, 1:2], in_=s.t[:, 0:1], func=AF.Ln, scale=1.0 / 512, bias=EPS),
                reads=[s.b], writes=[s.b])
            P.op("act", lambda e, s=s: e.activation(
                out=s.t[:, 2:3], in_=s.t[:, 1:2], func=AF.Exp, scale=-0.5), reads=[s.b], writes=[s.b])
            c_ = cn.next()
            P.op("dve", lambda e, c_=c_, p=p, s=s: e.tensor_scalar(
                out=c_.t[:], in0=p.t[:], scalar1=s.t[:, 2:3], scalar2=None, op0=ALU.mult),
                reads=[p.b, s.b], writes=[c_.b])
            pt = ptr.next()
            for j in range(4):
                P.op("pe", lambda e, pt=pt, j=j, c_=c_: e.transpose(
                    pt.t[:, j, :], c_.t[:, j * 128:(j + 1) * 128], self.ident.t[:]),
                    reads=[c_.b, self.ident.b], writes=[pt.b], sig=(j == 3))
            for j in range(4):
                dst = outT.t[:, j, blk * 128:(blk + 1) * 128]
                if j % 2 == 0:
                    P.op("act", lambda e, pt=pt, j=j, dst=dst: e.activation(
                        out=dst, in_=pt.t[:, j, :], func=AF.Copy, scale=gcol.t[:, j:j + 1]),
                        reads=[pt.b, gcol.b], writes=[outT.b])
                else:
                    P.op("dve", lambda e, pt=pt, j=j, dst=dst: e.tensor_scalar(
                        out=dst, in0=pt.t[:, j, :], scalar1=gcol.t[:, j:j + 1], scalar2=None, op0=ALU.mult),
                        reads=[pt.b, gcol.b], writes=[outT.b])

    def rope64(self, p1, p2, CS1, CS2, tsl, out, outb, rt):
        P = self.P
        t1, t2 = rt.next(), rt.next()
        P.op("dve", lambda e: e.tensor_tensor(out=t1.t[0:64, :], in0=p1.t[0:64, :], in1=CS1.t[:, tsl], op=ALU.mult),
             reads=[p1.b, CS1.b], writes=[t1.b])
        P.op("dve", lambda e: e.tensor_tensor(out=t2.t[0:64, :], in0=p2.t[0:64, :], in1=CS2.t[:, tsl], op=ALU.mult),
             reads=[p2.b, CS2.b], writes=[t2.b])
        P.op("dve", lambda e: e.tensor_tensor(out=out, in0=t1.t[0:64, :], in1=t2.t[0:64, :], op=ALU.add),
             reads=[t1.b, t2.b], writes=[outb])

    def shared_kv(self):
        nc, P = self.nc, self.P
        P.phase = 'kv'
        wdkv = self.w_dkv.rearrange("(kc p) n -> p kc n", p=128)
        self.ck_src = [self.dscr(f"cks{a}", [64, 4 * TOK], BF16) for a in range(2)]
        self.ck_dst = [self.dscr(f"ckd{a}", [128, 4 * TOK], BF16) for a in range(2)]
        self.kr_src = self.dscr("krs", [64, TOK], BF16)
        self.kr_dst = self.dscr("krd", [128, TOK], BF16)
        self.ck_src_b = [Buf(), Buf()]
        self.ck_dst_b = [Buf(), Buf()]
        self.kr_src_b, self.kr_dst_b = Buf(), Buf()
        csem = P.dsem()
        with contextlib.ExitStack() as es:
            hT = self.slot(es, "hT", [128, KC, TOK], BF16)
            with contextlib.ExitStack() as es2:
                self.prologue(es2, 3, self.xs, self.xs_b, hT)
                P.barrier()
                P.emit()
            CS1, CS2 = self.trig(es, 64, 1, [2, 3], ["CS1", "CS2"])
            wd = self.slot(es, "wd", [128, KC, 512], BF16, dma=True)
            wk1 = self.slot(es, "wk1", [128, KC, 64], BF16, dma=True)
            wk2 = self.slot(es, "wk2", [128, KC, 64], BF16, dma=True)
            gcol = self.slot(es, "kvg", [128, 4], F32, dma=True)
            ckvT = self.slot(es, "ckvT", [128, 4, TOK], BF16, dma=True)
            krT = self.slot(es, "krT", [64, TOK], BF16, dma=True)
            rt = self.ring(es, "krt", 4, [128, 512], F32)
            pp = self.ring(es, "kpp", 4, [128, 512], F32, psum=True)
            ptr = self.ring(es, "kpt", 2, [128, 4, 128], BF16, psum=True)
            self.dma("pool", wd.t[:], wdkv[:, :, 0:512], [], [wd.b], wd.sem)
            for (w, c0) in ((wk1, 512), (wk2, 544)):
                for d in range(2):
                    self.dma("pool", w.t[:, :, d * 32:(d + 1) * 32], wdkv[:, :, c0:c0 + 32], [], [w.b], w.sem)
            self.dma("sp", gcol.t[:], self.kvlat_col, [], [gcol.b], gcol.sem)
            self.latent_proj(es, hT, wd, gcol, ckvT, pp, ptr)
            for tt in range(4):
                tsl = slice(tt * 512, (tt + 1) * 512)
                p1, p2 = pp.next(), pp.next()
                for (w, p) in ((wk1, p1), (wk2, p2)):
                    for kc in range(KC):
                        P.op("pe", lambda e, p=p, w=w, kc=kc, tsl=tsl: e.matmul(
                            p.t[0:64, :], w.t[:, kc, :], hT.t[:, kc, tsl], start=(kc == 0), stop=(kc == KC - 1)),
                            reads=[w.b, hT.b], writes=[p.b], sig=(kc == KC - 1))
                self.rope64(p1, p2, CS1, CS2, tsl, krT.t[:, tsl], krT.b, rt)
            st_sems = [ckvT.sem, P.dsem()]
            es.callback(P.release, [st_sems[1]])
            for a in range(2):
                self.dma("sp", self.ck_src[a].rearrange("p (c n) -> p c n", c=4), ckvT.t[a * 64:(a + 1) * 64, :, :],
                         [ckvT.b], [self.ck_src_b[a]], st_sems[a])
            self.dma("sp", self.kr_src, krT.t[:], [krT.b], [self.kr_src_b], krT.sem)
            for (src, dst, sb_, db_) in ((self.ck_src[0], self.ck_dst[0], self.ck_src_b[0], self.ck_dst_b[0]),
                                         (self.ck_src[1], self.ck_dst[1], self.ck_src_b[1], self.ck_dst_b[1]),
                                         (self.kr_src, self.kr_dst, self.kr_src_b, self.kr_dst_b)):
                P.op("pool", lambda e, src=src, dst=dst: e.collective_compute(
                    "AllGather", ALU.bypass, replica_groups=PAIRS, ins=[src], outs=[dst]),
                    reads=[sb_], writes=[db_], dsem=csem, inc=None)
            P.barrier()
            P.emit()
        P.release([csem])

    def mla(self):
        nc, P = self.nc, self.P
        P.phase = 'mla'
        SCALE = 192.0 ** -0.5
        ao = self.dscr("ao", [TOK, D], BF16)
        ao_b = [Buf() for _ in range(NBLK)]
        wukv = self.w_ukv.rearrange("(c p) n -> p c n", p=128)
        wuq = self.w_uq[0].rearrange("(c p) n -> p c n", p=128)
        with contextlib.ExitStack() as es:
            cqT = self.slot(es, "cqT", [128, 4, TOK], BF16)
            with contextlib.ExitStack() as esA:
                hT = self.slot(esA, "hT", [128, KC, TOK], BF16)
                with contextlib.ExitStack() as es2:
                    self.prologue(es2, 5, self.xs, self.xs_b, hT)
                    P.barrier()
                    P.emit()
                wd = self.slot(esA, "wdq", [128, KC, 512], BF16, dma=True)
                gcol = self.slot(esA, "qg", [128, 4], F32, dma=True)
                pp = self.ring(esA, "qpp", 3, [128, 512], F32, psum=True)
                ptr = self.ring(esA, "qpt", 2, [128, 4, 128], BF16, psum=True)
                self.dma("pool", wd.t[:], self.w_dq[0].rearrange("(kc p) n -> p kc n", p=128), [], [wd.b], wd.sem)
                self.dma("sp", gcol.t[:], self.qlat_col, [], [gcol.b], gcol.sem)
                self.latent_proj(esA, hT, wd, gcol, cqT, pp, ptr)
                P.barrier()
                P.emit()
            CS1, CS2 = self.trig(es, 64, 1, [2, 3], ["qCS1", "qCS2"])
            ckvA = self.slot(es, "ckvA", [128, 4, 2 * TOK], BF16, dma=True)
            krA = self.slot(es, "krA", [64, 2 * TOK], BF16, dma=True)
            for a in range(2):
                self.dma("sp", ckvA.t[a * 64:(a + 1) * 64, :, 0:TOK],
                         self.ck_dst[a][0:64, :].rearrange("p (c n) -> p c n", c=4),
                         [self.ck_dst_b[a]], [ckvA.b], ckvA.sem)
                self.dma("sp", ckvA.t[a * 64:(a + 1) * 64, :, TOK:2 * TOK],
                         self.ck_src[a].rearrange("p (c n) -> p c n", c=4),
                         [self.ck_src_b[a]], [ckvA.b], ckvA.sem)
            self.dma("sp", krA.t[:, 0:TOK], self.kr_dst[0:64, :], [self.kr_dst_b], [krA.b], krA.sem)
            self.dma("sp", krA.t[:, TOK:2 * TOK], self.kr_src, [self.kr_src_b], [krA.b], krA.sem)
            msk = self.slot(es, "msk", [128, 4, 512], F32, dma=True)
            self.dma("sp", msk.t[:], self.c_mlamask, [], [msk.b], msk.sem)
            wk = self.ring(es, "wuk", 2, [128, 4, 128], BF16, dma=True)
            wv = self.ring(es, "wuv", 2, [128, 4, 128], BF16, dma=True)
            wqn = self.ring(es, "wqn", 2, [128, 4, 128], BF16, dma=True)
            wq1 = self.ring(es, "wq1", 2, [128, 4, 64], BF16, dma=True)
            wq2 = self.ring(es, "wq2", 2, [128, 4, 64], BF16, dma=True)
            knTr = self.ring(es, "knT", 2, [128, 2 * TOK], BF16)
            vhr = self.ring(es, "vh", 2, [128, 32, 130], BF16)
            qnTr = self.ring(es, "qnT", 2, [128, TOK], BF16)
            qrTr = self.ring(es, "qrT", 2, [64, TOK], BF16)
            PT = self.ring(es, "PT", 6, [128, 512], BF16)
            mtmp = self.ring(es, "mtmp", 3, [128, 512], F32)
            stt = self.ring(es, "mst", 4, [128, 4], F32)
            osb = self.ring(es, "osb", 4, [128, 128], BF16, dma=True)
            rt = self.ring(es, "mrt", 4, [128, 512], F32)
            pp = self.ring(es, "mpp", 4, [128, 512], F32, psum=True)
            por = self.ring(es, "mpo", 4, [128, 512], F32, psum=True)
            for sl in vhr.slots:
                P.op("dve", lambda e, sl=sl: e.memset(sl.t[:, :, 128:130], 1.0), writes=[sl.b])
            cnt = [0]

            def evac_copy(dst, src, reads, writes):
                cnt[0] += 1
                if cnt[0] % 2 == 0:
                    P.op("act", lambda e: e.copy(out=dst, in_=src), reads=reads, writes=writes)
                else:
                    P.op("dve", lambda e: e.tensor_copy(out=dst, in_=src), reads=reads, writes=writes)

            def head_proj(h):
                wk_h, wv_h, wqn_h, wq1_h, wq2_h = wk.next(), wv.next(), wqn.next(), wq1.next(), wq2.next()
                knT, vh, qnT, qrT = knTr.next(), vhr.next(), qnTr.next(), qrTr.next()
                self.dma("pool", wk_h.t[:], wukv[:, :, h * 256:h * 256 + 128], [], [wk_h.b], wk_h.sem)
                self.dma("pool", wv_h.t[:], wukv[:, :, h * 256 + 128:h * 256 + 256], [], [wv_h.b], wv_h.sem)
                self.dma("pool", wqn_h.t[:], wuq[:, :, h * 192:h * 192 + 128], [], [wqn_h.b], wqn_h.sem)
                for (w, c0) in ((wq1_h, h * 192 + 128), (wq2_h, h * 192 + 160)):
                    for d in range(2):
                        self.dma("pool", w.t[:, :, d * 32:(d + 1) * 32], wuq[:, :, c0:c0 + 32], [], [w.b], w.sem)
                for tt in range(8):
                    p = pp.next()
                    for c in range(4):
                        P.op("pe", lambda e, p=p, c=c, tt=tt: e.matmul(
                            p.t[:], wk_h.t[:, c, :], ckvA.t[:, c, tt * 512:(tt + 1) * 512],
                            start=(c == 0), stop=(c == 3)),
                            reads=[wk_h.b, ckvA.b], writes=[p.b], sig=(c == 3))
                    evac_copy(knT.t[:, tt * 512:(tt + 1) * 512], p.t[:], [p.b], [knT.b])
                for b4 in range(8):
                    p = pp.next()
                    for j in range(4):
                        blk = b4 * 4 + j
                        for c in range(4):
                            P.op("pe", lambda e, p=p, c=c, j=j, blk=blk: e.matmul(
                                p.t[:, j * 128:(j + 1) * 128], ckvA.t[:, c, blk * 128:(blk + 1) * 128],
                                wv_h.t[:, c, :], start=(c == 0), stop=(c == 3)),
                                reads=[wv_h.b, ckvA.b], writes=[p.b], sig=(c == 3 and j == 3))
                    evac_copy(vh.t[:, b4 * 4:(b4 + 1) * 4, 0:128], p.t[:].rearrange("p (a n) -> p a n", a=4),
                              [p.b], [vh.b])
                for tt in range(4):
                    tsl = slice(tt * 512, (tt + 1) * 512)
                    p = pp.next()
                    for c in range(4):
                        P.op("pe", lambda e, p=p, c=c, tsl=tsl: e.matmul(
                            p.t[:], wqn_h.t[:, c, :], cqT.t[:, c, tsl], start=(c == 0), stop=(c == 3)),
                            reads=[wqn_h.b, cqT.b], writes=[p.b], sig=(c == 3))
                    evac_copy(qnT.t[:, tsl], p.t[:], [p.b], [qnT.b])
                    p1, p2 = pp.next(), pp.next()
                    for (w, p_) in ((wq1_h, p1), (wq2_h, p2)):
                        for c in range(4):
                            P.op("pe", lambda e, p_=p_, w=w, c=c, tsl=tsl: e.matmul(
                                p_.t[0:64, :], w.t[:, c, :], cqT.t[:, c, tsl], start=(c == 0), stop=(c == 3)),
                                reads=[w.b, cqT.b], writes=[p_.b], sig=(c == 3))
                    self.rope64(p1, p2, CS1, CS2, tsl, qrT.t[:, tsl], qrT.b, rt)
                return knT, vh, qnT, qrT

            def attention(h, knT, vh, qnT, qrT):
                for g in range(4):
                    gsl = slice(g * 512, (g + 1) * 512)
                    kbs = list(range(16)) + [16 + j for j in range(4 * g + 4)]
                    pos_ = [por.next() for _ in range(4)]
                    pts = {}

                    def S(i):
                        kb = kbs[i]
                        ks = slice(kb * 128, (kb + 1) * 128)
                        p = pp.next()
                        P.op("pe", lambda e, p=p, ks=ks, gsl=gsl: e.matmul(
                            p.t[:], knT.t[:, ks], qnT.t[:, gsl], start=True, stop=False),
                            reads=[knT.b, qnT.b], writes=[p.b], sig=False)
                        P.op("pe", lambda e, p=p, ks=ks, gsl=gsl: e.matmul(
                            p.t[:], krA.t[:, ks], qrT.t[:, gsl], start=False, stop=True),
                            reads=[krA.b, qrT.b], writes=[p.b])
                        pt = PT.next()
                        pts[i] = pt
                        if kb < 16:
                            P.op("act", lambda e, p=p, pt=pt: e.activation(
                                out=pt.t[:], in_=p.t[:], func=AF.Exp, scale=SCALE, bias=self.flag_sb.t[:, 1:2]),
                                reads=[p.b, self.flag_sb.b], writes=[pt.b])
                        elif kb - 16 < 4 * g:
                            P.op("act", lambda e, p=p, pt=pt: e.activation(
                                out=pt.t[:], in_=p.t[:], func=AF.Exp, scale=SCALE),
                                reads=[p.b], writes=[pt.b])
                        else:
                            jj = kb - 16 - 4 * g
                            m_ = mtmp.next()
                            P.op("dve", lambda e, p=p, m_=m_, jj=jj: e.scalar_tensor_tensor(
                                out=m_.t[:], in0=p.t[:], scalar=SCALE, in1=msk.t[:, jj, :],
                                op0=ALU.mult, op1=ALU.add), reads=[p.b, msk.b], writes=[m_.b])
                            P.op("act", lambda e, m_=m_, pt=pt: e.activation(
                                out=pt.t[:], in_=m_.t[:], func=AF.Exp), reads=[m_.b], writes=[pt.b])

                    def PV(i):
                        kb = kbs[i]
                        pt = pts.pop(i)
                        for t in range(4):
                            P.op("pe", lambda e, t=t, pt=pt, kb=kb, i=i, po_=pos_[t], last=(i == len(kbs) - 1): e.matmul(
                                po_.t[:, 0:130], pt.t[:, t * 128:(t + 1) * 128], vh.t[:, kb, :],
                                start=(i == 0), stop=last),
                                reads=[pt.b, vh.b], writes=[pos_[t].b], sig=(i == len(kbs) - 1 or t == 3))

                    LAG = 4
                    for i in range(len(kbs) + LAG):
                        if i < len(kbs):
                            S(i)
                        if i >= LAG:
                            PV(i - LAG)
                    for t in range(4):
                        st = stt.next()
                        P.op("dve", lambda e, st=st, t=t: e.reciprocal(out=st.t[:, 0:1], in_=pos_[t].t[:, 128:129]),
                             reads=[pos_[t].b], writes=[st.b])
                        o_ = osb.next()
                        P.op("act", lambda e, o_=o_, st=st, t=t: e.activation(
                            out=o_.t[:], in_=pos_[t].t[:, 0:128], func=AF.Copy, scale=st.t[:, 0:1]),
                            reads=[pos_[t].b, st.b], writes=[o_.b])
                        qi = g * 4 + t
                        self.dma("sp", ao[qi * 128:(qi + 1) * 128, h * 128:(h + 1) * 128], o_.t[:],
                                 [o_.b], [ao_b[qi]], o_.sem)

            nxt = head_proj(0)
            for h in range(MLA_H):
                cur = nxt
                if h + 1 < MLA_H:
                    nxt = head_proj(h + 1)
                attention(h, *cur)
            P.barrier()
            P.emit()
        with contextlib.ExitStack() as es:
            gate = self.slot(es, "gate", [128, D], F32, dma=True)
            self.dma("sp", gate.t[:], self.gates[4:5, :].partition_broadcast(128), [self.gates_b], [gate.b], gate.sem)
            self.tm_outproj(es, ao, ao_b, 16, lambda dc: self.mla_w_out[0][:, dc * 256:(dc + 1) * 256], gate, 1.0)

    def final_out(self):
        nc, P = self.nc, self.P
        P.phase = 'final'
        src, sbufs = (self.x_in, self.xin_b) if self.stage < 1 else (self.xs, self.xs_b)
        with contextlib.ExitStack() as es:
            fg = Slot(self.sb(es, "fg", [128, D]), P.dsem())
            xin = self.ring(es, "fx", 4, [128, D], F32, dma=True)
            junk = Slot(self.sb(es, "fjunk", [128, D], BF16))
            st = self.ring(es, "fst", 4, [128, 4], F32)
            ob = [Buf() for _ in range(NBLK)]
            if self.stage >= 8:
                self.dma("sp", fg.t[:], self.final_g.partition_broadcast(128), [], [fg.b], fg.sem)
            xts = {}

            def load(blk):
                xt = xin.next()
                xts[blk] = xt
                self.dma("sp", xt.t[:], src[blk * 128:(blk + 1) * 128, :], [sbufs[blk]], [xt.b], xt.sem)

            for blk in range(min(3, NBLK)):
                load(blk)
            for blk in range(NBLK):
                xt = xts.pop(blk)
                if self.stage >= 8:
                    s = st.next()
                    P.op("act", lambda e, xt=xt, s=s: e.activation(
                        out=junk.t[:], in_=xt.t[:], func=AF.Square, accum_out=s.t[:, 0:1]),
                        reads=[xt.b], writes=[junk.b, s.b])
                    P.op("act", lambda e, s=s: e.activation(
                        out=s.t[:, 1:2], in_=s.t[:, 0:1], func=AF.Ln, scale=1.0 / D, bias=EPS),
                        reads=[s.b], writes=[s.b])
                    P.op("act", lambda e, s=s: e.activation(
                        out=s.t[:, 2:3], in_=s.t[:, 1:2], func=AF.Exp, scale=-0.5),
                        reads=[s.b], writes=[s.b])
                    P.op("dve", lambda e, xt=xt, s=s: e.scalar_tensor_tensor(
                        out=xt.t[:], in0=xt.t[:], scalar=s.t[:, 2:3], in1=fg.t[:], op0=ALU.mult, op1=ALU.mult),
                        reads=[xt.b, s.b, fg.b], writes=[xt.b])
                if blk + 3 < NBLK:
                    load(blk + 3)
                self.dma("sp", self.out[blk * 128:(blk + 1) * 128, :], xt.t[:], [xt.b], [ob[blk]], xt.sem)
            P.barrier()
            P.emit()


def _consts():
    bf = ml_dtypes.bfloat16
    c = {}
    c["c_ident"] = np.eye(128, dtype=np.float32).astype(bf)
    invf = np.zeros((128, 4), np.float32)
    invf[:, 0] = 10000.0 ** (-np.arange(128, dtype=np.float32) / 128)
    inv32 = 10000.0 ** (-np.arange(32, dtype=np.float32) / 32)
    invf[:64, 1] = np.concatenate([inv32, inv32])
    invf[:32, 2] = PI / 2
    invf[32:64, 2] = 0.0
    invf[:32, 3] = PI
    invf[32:64, 3] = PI / 2
    c["c_invf"] = invf
    log_g = np.log1p(-(2.0 ** (-5.0 - np.arange(RET_H, dtype=np.float32)))).astype(np.float32)
    n = np.arange(128, dtype=np.float32)
    chunk = (np.arange(128) // 64)
    dT = np.zeros((128, RET_H, 128), np.float32)
    for h in range(RET_H):
        d = np.exp(log_g[h] * np.abs(n[:, None] - n[None, :]))
        vis = (chunk[:, None] <= chunk[None, :])
        dT[:, h, :] = d * vis / 16.0
    c["c_dT"] = dT
    c["c_zeta"] = np.stack([np.exp(log_g[h] * (127.0 - n)) for h in range(RET_H)], 1).astype(np.float32)
    xi = (np.stack([np.exp(log_g[h] * (n + 1.0)) for h in range(RET_H)], 0) / 16.0).astype(np.float32)
    c["c_xi"] = np.broadcast_to(xi[None], (128, RET_H, 128)).copy()
    q_chunk = chunk[:, None]
    k_chunk = chunk[None, :]
    c["c_diag"] = np.where(k_chunk <= q_chunk, 0.0, -1e30).astype(np.float32)
    mm = np.zeros((128, 4, 512), np.float32)
    for jj in range(4):
        for t in range(4):
            if t < jj:
                mm[:, jj, t * 128:(t + 1) * 128] = -1e30
            elif t == jj:
                mm[:, jj, t * 128:(t + 1) * 128] = np.where(chunk[:, None] <= chunk[None, :], 0.0, -1e30)
    c["c_mlamask"] = mm
    c["_g128"] = [float(np.exp(log_g[h] * 128.0)) for h in range(RET_H)]
    return c


def _col(v):
    return np.ascontiguousarray(v.reshape(-1, 128).T)


def make_in_maps(inp, ncores=8):
    cst = _consts()
    shared = {
        "ffn_w_in": inp["ffn_w_in"],
        "ffn_w_out": inp["ffn_w_out"], "ret_w_in": inp["ret_w_in"], "ret_gn_g": inp["ret_gn_g"],
        "ret_w_out": inp["ret_w_out"], "mla_w_dkv": inp["mla_w_dkv"],
        "mla_w_ukv": inp["mla_w_ukv"], "mla_w_dq": inp["mla_w_dq"], "mla_w_uq": inp["mla_w_uq"],
        "mla_w_out": inp["mla_w_out"], "final_g": inp["final_g"].reshape(1, -1),
        "kvlat_col": _col(inp["kv_latent_g"]), "qlat_col": _col(inp["q_latent_g"][0]),
    }
    ng = inp["norm_g"]
    sites = [ng[0, 0], ng[0, 1], ng[0, 2], inp["kv_norm_g"], ng[1, 0], ng[1, 1], ng[1, 2]]
    shared["normg_col"] = np.ascontiguousarray(np.stack([_col(s) for s in sites], 1))
    for k, v in cst.items():
        if not k.startswith("_"):
            shared[k] = v
    wsrc = [(inp["ada_w"][0], inp["ada_b"][0]), (inp["kv_ada_w"], inp["kv_ada_b"]), (inp["ada_w"][1], inp["ada_b"][1])]
    bounds = np.cumsum([0] + [w.shape[1] for w, _ in wsrc])
    per = 20480

    def col_slice(c0, c1):
        ws, bs = [], []
        for k, (w, bv) in enumerate(wsrc):
            lo, hi = max(c0, bounds[k]), min(c1, bounds[k + 1])
            if lo < hi:
                ws.append(w[:, lo - bounds[k]:hi - bounds[k]])
                bs.append(bv[lo - bounds[k]:hi - bounds[k]])
        return np.ascontiguousarray(np.concatenate(ws, 1)), np.ascontiguousarray(np.concatenate(bs)[None, :])

    halves = [col_slice(0, per), col_slice(per, 2 * per)]
    maps = []
    for i in range(ncores):
        b, half = i // 2, i % 2
        m = dict(shared)
        m["mods_w"], m["mods_b"] = halves[half]
        m["c_col"] = _col(inp["c"][b])
        m["x_in"] = np.ascontiguousarray(inp["x"][b, half * TOK:(half + 1) * TOK, :])
        m["pos"] = np.ascontiguousarray(inp["positions"][b, half * TOK:(half + 1) * TOK].reshape(1, TOK)).astype(np.int32)
        fl = np.zeros((128, 2), np.float32)
        fl[:, 0] = float(half)
        fl[:, 1] = 0.0 if half == 1 else -1e30
        m["flag"] = fl
        maps.append(m)
    return maps


_NC_CACHE = {}


def kernel(**inputs):
    inp = {k: np.asarray(v) for k, v in inputs.items()}
    if "full" not in _NC_CACHE:
        _NC_CACHE["full"] = K(stage=99).build()
    nc = _NC_CACHE["full"]
    maps = make_in_maps(inp)
    res = run_bass_kernel_spmd(nc, maps, core_ids=list(range(8)))
    out = np.empty((4, SEQ, D), np.float32)
    for i in range(8):
        b, half = i // 2, i % 2
        out[b, half * TOK:(half + 1) * TOK, :] = np.asarray(res.results[i]["out"])
    return out
```

```python
import contextlib
import math

import ml_dtypes
import numpy as np

import concourse.bass as bass
import concourse.mybir as mybir
from concourse.bass_utils import run_bass_kernel_spmd

F32 = mybir.dt.float32
BF16 = mybir.dt.bfloat16
I32 = mybir.dt.int32
AF = mybir.ActivationFunctionType
ALU = mybir.AluOpType
AX = mybir.AxisListType

D = 2048
SEQ = 4096
TOK = 2048
NBLK = TOK // 128
KC = D // 128
DFF = 5632
EPS = 1e-6
RET_H = 8
MLA_H = 16
PI = math.pi
PAIRS = [[0, 1], [2, 3], [4, 5], [6, 7]]
NCORES = 8


class S:
    def __init__(self, h):
        self.h = h
        self.count = 0


class Buf:
    __slots__ = ("w", "r")

    def __init__(self):
        self.w = None
        self.r = {}


class EngRec:
    def __init__(self, name, sem):
        self.name = name
        self.sem = sem
        self.ops = []
        self.waited = {}


class Prog:
    ENG = ("pe", "act", "dve", "pool", "sp")

    def __init__(self, nc, es, nsem=56):
        self.nc = nc
        self.pool = [es.enter_context(nc.semaphore(f"s{i}")) for i in range(nsem)]
        self.eng = {n: EngRec(n, S(self.pool.pop())) for n in self.ENG}
        self.dsems = []
        self.free = []

    def dsem(self):
        if self.free:
            return self.free.pop()
        s = S(self.pool.pop())
        self.dsems.append(s)
        return s

    def release(self, sems):
        self.free.extend(sems)

    def op(self, eng, fn, reads=(), writes=(), sig=True, dsem=None, inc=16):
        e = self.eng[eng]
        need = {}

        def add(s, v):
            if eng == "pe" and s is e.sem:
                return
            if need.get(s, 0) < v:
                need[s] = v

        for b in reads:
            if b.w is not None:
                add(*b.w)
        for b in writes:
            if b.w is not None:
                add(*b.w)
            for s, v in b.r.items():
                add(s, v)
        waits = []
        for s, v in need.items():
            if e.waited.get(s, 0) >= v:
                continue
            e.waited[s] = v
            waits.append((s, v))
        if dsem is not None:
            dsem.count += (1 if inc is None else inc)
            ev = (dsem, dsem.count)
            sig_t = (dsem, inc)
        elif sig:
            e.sem.count += 1
            ev = (e.sem, e.sem.count)
            sig_t = (e.sem, 1)
        else:
            ev = (e.sem, e.sem.count + 1)
            sig_t = None
        for b in reads:
            if b.r.get(ev[0], 0) < ev[1]:
                b.r[ev[0]] = ev[1]
        for b in writes:
            b.w = ev
            b.r = {}
        e.ops.append((waits, fn, sig_t))
        return ev

    def barrier(self):
        for n in self.ENG:
            e = self.eng[n]
            waits = []
            for m in self.ENG:
                s = self.eng[m].sem
                if m != n and s.count > e.waited.get(s, 0):
                    e.waited[s] = s.count
                    waits.append((s, s.count))
            for s in self.dsems:
                if s.count > e.waited.get(s, 0):
                    e.waited[s] = s.count
                    waits.append((s, s.count))
            if waits:
                e.ops.append((waits, None, None))

    def emit(self):
        nc = self.nc
        self.nemit = getattr(self, "nemit", 0) + 1
        with nc.named_scope(f"{getattr(self, 'phase', 'ph')}_{self.nemit}"), nc.Block() as block:
            def mk(name):
                e = self.eng[name]

                def body(engobj):
                    for waits, fn, sig_t in e.ops:
                        for s, v in waits:
                            engobj.wait_ge(s.h, v)
                        if fn is None:
                            continue
                        ins = fn(engobj)
                        if sig_t is not None:
                            if sig_t[1] is None:
                                ins.then_inc(sig_t[0].h)
                            else:
                                ins.then_inc(sig_t[0].h, sig_t[1])
                    e.ops = []
                return body
            reg = {"pe": block.tensor, "act": block.scalar, "dve": block.vector,
                   "pool": block.gpsimd, "sp": block.sync}
            for n in self.ENG:
                if self.eng[n].ops:
                    reg[n](mk(n))


class Slot:
    def __init__(self, t, sem=None):
        self.t = t
        self.b = Buf()
        self.sem = sem


class Ring:
    def __init__(self, slots):
        self.slots = slots
        self.i = 0

    def next(self):
        s = self.slots[self.i % len(self.slots)]
        self.i += 1
        return s


class K:
    def __init__(self, stage=99, debug=False):
        self.stage = stage
        self.debug = debug
        self.nc = bass.Bass("TRN2", target_bir_lowering=False)
        self.es = contextlib.ExitStack()
        self.uid = 0

    def din(self, name, shape, dt=F32):
        return self.nc.dram_tensor(name, list(shape), dt, kind="ExternalInput").ap()

    def dscr(self, name, shape, dt=F32):
        return self.nc.dram_tensor(name, list(shape), dt, kind="Internal").ap()

    def sb(self, es, name, shape, dt=F32):
        self.uid += 1
        return es.enter_context(self.nc.sbuf_tensor(f"{name}_u{self.uid}", list(shape), dt))

    def ps(self, es, name, shape, dt=F32):
        self.uid += 1
        return es.enter_context(self.nc.psum_tensor(f"{name}_u{self.uid}", list(shape), dt))

    def ring(self, es, name, n, shape, dt=F32, dma=False, psum=False):
        mk = self.ps if psum else self.sb
        self.uid += 1
        slots = [Slot(mk(es, f"{name}_{self.uid}_{i}", shape, dt), self.P.dsem() if dma else None)
                 for i in range(n)]
        if dma:
            es.callback(self.P.release, [sl.sem for sl in slots])
        return Ring(slots)

    def slot(self, es, name, shape, dt=F32, dma=False, psum=False):
        return self.ring(es, name, 1, shape, dt, dma, psum).slots[0]

    def dma(self, q, out, in_, reads, writes, sem):
        self.P.op(q, lambda e: e.dma_start(out=out, in_=in_), reads=reads, writes=writes, dsem=sem)

    def build(self):
        nc, es = self.nc, self.es
        with es:
            self.P = P = Prog(nc, es)
            self.declare_io()
            self.setup_consts()
            self.phase_mods()
            x_src, xb_src = self.x_in, self.xin_b
            if self.stage >= 1:
                self.ffn(0, 0, site=0, grow=0, x_src=self.x_in, xsb=self.xin_b)
            if self.stage >= 2:
                self.retention()
            if self.stage >= 3:
                self.ffn(0, 1, site=2, grow=2, x_src=self.xs, xsb=self.xs_b)
            if self.stage >= 4:
                self.shared_kv()
            if self.stage >= 5:
                self.ffn(1, 0, site=4, grow=3, x_src=self.xs, xsb=self.xs_b)
            if self.stage >= 6:
                self.mla()
            if self.stage >= 7:
                self.ffn(1, 1, site=6, grow=5, x_src=self.xs, xsb=self.xs_b)
            self.final_out()
        return nc

    def declare_io(self):
        self.x_in = self.din("x_in", [TOK, D])
        self.c_col = self.din("c_col", [128, KC])
        self.pos = self.din("pos", [1, TOK], I32)
        self.flag = self.din("flag", [128, 2])
        self.mods_w = self.din("mods_w", [D, 20480])
        self.mods_b = self.din("mods_b", [1, 20480])
        self.normg_col = self.din("normg_col", [128, 7, KC])
        self.ffn_w_in = self.din("ffn_w_in", [2, 2, D, 2 * DFF])
        self.ffn_w_out = self.din("ffn_w_out", [2, 2, DFF, D])
        self.ret_w_in = self.din("ret_w_in", [1, D, 12288])
        self.ret_gn_g = self.din("ret_gn_g", [1, 4096])
        self.ret_w_out = self.din("ret_w_out", [1, 4096, D])
        self.w_dkv = self.din("mla_w_dkv", [D, 576])
        self.kvlat_col = self.din("kvlat_col", [128, 4])
        self.w_ukv = self.din("mla_w_ukv", [512, 4096])
        self.w_dq = self.din("mla_w_dq", [1, D, 512])
        self.qlat_col = self.din("qlat_col", [128, 4])
        self.w_uq = self.din("mla_w_uq", [1, 512, 3072])
        self.mla_w_out = self.din("mla_w_out", [1, D, D])
        self.final_g = self.din("final_g", [1, D])
        self.c_ident = self.din("c_ident", [128, 128], BF16)
        self.c_invf = self.din("c_invf", [128, 4])
        self.c_dT = self.din("c_dT", [128, RET_H, 128])
        self.c_zeta = self.din("c_zeta", [128, RET_H])
        self.c_xi = self.din("c_xi", [128, RET_H, 128])
        self.c_diag = self.din("c_diag", [128, 128])
        self.c_mlamask = self.din("c_mlamask", [128, 4, 512])
        self.out = self.nc.dram_tensor("out", [TOK, D], F32, kind="ExternalOutput").ap()
        self.xs = self.dscr("xs", [TOK, D])
        self.gates = self.dscr("gates", [6, D])
        self.xin_b = [Buf() for _ in range(NBLK)]
        self.xs_b = [Buf() for _ in range(NBLK)]
        self.gates_b = Buf()
        if self.debug:
            self.dbg = self.nc.dram_tensor("dbg", [128, 4096], F32, kind="ExternalOutput").ap()

    def setup_consts(self):
        nc, P, es = self.nc, self.P, self.es
        self.ident = Slot(self.sb(es, "ident", [128, 128], BF16), P.dsem())
        self.modcol = Slot(self.sb(es, "modcol", [128, 14, KC]))
        self.Acol = Slot(self.sb(es, "Acol", [128, 7, KC]))
        self.Bcol = Slot(self.sb(es, "Bcol", [128, 7, KC]))
        self.flag_sb = Slot(self.sb(es, "flag_sb", [128, 2]), P.dsem())
        self.dma("sp", self.ident.t[:], self.c_ident, [], [self.ident.b], self.ident.sem)
        self.dma("sp", self.flag_sb.t[:], self.flag, [], [self.flag_sb.b], self.flag_sb.sem)

    def phase_mods(self):
        self.P.phase = 'mods'
        nc, P = self.nc, self.P
        NCH = 40
        msrc = self.dscr("msrc", [1, NCH * 512])
        mdst = self.dscr("mdst", [2, NCH * 512])
        with contextlib.ExitStack() as es:
            ccol = self.slot(es, "ccol", [128, KC], F32, dma=True)
            ngc = self.slot(es, "ngc", [128, 7, KC], F32, dma=True)
            cact = self.slot(es, "cact", [128, KC], BF16)
            one = self.slot(es, "one", [1, 1])
            loc = self.slot(es, "mloc", [1, NCH * 512], F32, dma=True)
            wring = self.ring(es, "mw", 2, [128, KC, 512], BF16, dma=True)
            bring = self.ring(es, "mb", 2, [1, 512], F32, dma=True)
            rring = self.ring(es, "mr", 4, [1, 512], F32, dma=True)
            gsem = self.ring(es, "mgs", 4, [1, 2], F32, dma=True)
            pring = self.ring(es, "mp", 2, [128, 512], F32, psum=True)
            cring = self.ring(es, "mc", 2, [128, 4], F32, psum=True)
            csem = P.dsem()
            self.dma("sp", ccol.t[:], self.c_col, [], [ccol.b], ccol.sem)
            self.dma("sp", ngc.t[:], self.normg_col, [], [ngc.b], ngc.sem)
            P.op("act", lambda e: e.activation(out=cact.t[:], in_=ccol.t[:], func=AF.Silu),
                 reads=[ccol.b], writes=[cact.b])
            P.op("dve", lambda e: e.memset(one.t[:], 1.0), writes=[one.b])
            wv = self.mods_w.rearrange("(kc p) n -> p kc n", p=128)
            for cc in range(NCH):
                wt = wring.next()
                self.dma("pool", wt.t[:], wv[:, :, cc * 512:(cc + 1) * 512], [], [wt.b], wt.sem)
                bt = bring.next()
                self.dma("sp", bt.t[:], self.mods_b[:, cc * 512:(cc + 1) * 512], [], [bt.b], bt.sem)
                pt = pring.next()
                for kc in range(KC):
                    P.op("pe", lambda e, kc=kc, pt=pt, wt=wt: e.matmul(
                        pt.t[0:1, :], cact.t[:, kc:kc + 1], wt.t[:, kc, :],
                        start=(kc == 0), stop=(kc == KC - 1)),
                        reads=[cact.b, wt.b], writes=[pt.b], sig=(kc == KC - 1))
                P.op("dve", lambda e, pt=pt, bt=bt, cc=cc: e.tensor_tensor(
                    out=loc.t[:, cc * 512:(cc + 1) * 512], in0=pt.t[0:1, :], in1=bt.t[:], op=ALU.add),
                    reads=[pt.b, bt.b], writes=[loc.b])
            sb_, db_ = Buf(), Buf()
            self.dma("sp", msrc, loc.t[:], [loc.b], [sb_], loc.sem)
            P.op("pool", lambda e: e.collective_compute(
                "AllGather", ALU.bypass, replica_groups=PAIRS, ins=[msrc], outs=[mdst]),
                reads=[sb_], writes=[db_], dsem=csem, inc=None)
            kinds_all = []
            for l in range(2):
                base = 6 * l + (2 if l == 1 else 0)
                kinds = [("col", base + 0), ("col", base + 1), ("gate", 3 * l + 0),
                         ("col", base + 2), ("col", base + 3), ("gate", 3 * l + 1),
                         ("col", base + 4), ("col", base + 5), ("gate", 3 * l + 2)]
                if l == 1:
                    kinds_all += [("col", 6), ("col", 7)]
                kinds_all += kinds
            gc = 0
            for (kind, idx) in kinds_all:
                for cc in range(4):
                    r, lc = divmod(gc, NCH)
                    gc += 1
                    srow = mdst[r:r + 1, lc * 512:(lc + 1) * 512]
                    if kind == "gate":
                        gs = gsem.next()
                        self.dma("sp", self.gates[idx:idx + 1, cc * 512:(cc + 1) * 512], srow,
                                 [db_], [self.gates_b], gs.sem)
                    else:
                        rt = rring.next()
                        self.dma("sp", rt.t[:], srow, [db_], [rt.b], rt.sem)
                        ct = cring.next()
                        for j in range(4):
                            P.op("pe", lambda e, j=j, ct=ct, rt=rt: e.matmul(
                                ct.t[:, j:j + 1], rt.t[0:1, j * 128:(j + 1) * 128], one.t[:],
                                start=True, stop=True),
                                reads=[rt.b, one.b], writes=[ct.b], sig=(j == 3))
                        P.op("dve", lambda e, ct=ct, idx=idx, cc=cc: e.tensor_copy(
                            out=self.modcol.t[:, idx, cc * 4:(cc + 1) * 4], in_=ct.t[:]),
                            reads=[ct.b], writes=[self.modcol.b])
            es.callback(P.release, [csem])
            for site in range(7):
                sh, sc = 2 * site, 2 * site + 1
                P.op("dve", lambda e, site=site, sc=sc: e.scalar_tensor_tensor(
                    out=self.Acol.t[:, site, :], in0=self.modcol.t[:, sc, :], scalar=1.0,
                    in1=ngc.t[:, site, :], op0=ALU.add, op1=ALU.mult),
                    reads=[self.modcol.b, ngc.b], writes=[self.Acol.b])
                P.op("dve", lambda e, site=site, sh=sh: e.tensor_copy(
                    out=self.Bcol.t[:, site, :], in_=self.modcol.t[:, sh, :]),
                    reads=[self.modcol.b], writes=[self.Bcol.b])
            if self.debug and self.stage == 0:
                o = Slot(self.sb(es, "dbgo", [128, 4096]), P.dsem())
                P.op("dve", lambda e: e.memset(o.t[:], 0.0), writes=[o.b])
                P.op("dve", lambda e: e.tensor_copy(out=o.t[:, 0:7 * KC], in_=self.Acol.t[:].rearrange("p a b -> p (a b)")),
                     reads=[self.Acol.b], writes=[o.b])
                P.op("dve", lambda e: e.tensor_copy(out=o.t[:, 112:112 + 7 * KC], in_=self.Bcol.t[:].rearrange("p a b -> p (a b)")),
                     reads=[self.Bcol.b], writes=[o.b])
                self.dma("sp", self.dbg, o.t[:], [o.b], [Buf()], o.sem)
            P.barrier()
            P.emit()

    def prologue(self, es, site, x_src, xsb, hT, ptr_slots=None, nxin=3):
        nc, P = self.nc, self.P
        xin = self.ring(es, "pxin", nxin, [128, D], F32, dma=True)
        junk = Slot(self.sb(es, "pjunk", [128, D], BF16))
        xn = self.ring(es, "pxn", 2 if ptr_slots is not None else 3, [128, D], BF16)
        st = self.ring(es, "pst", 4, [128, 4], F32)
        ptr = Ring(ptr_slots) if ptr_slots is not None else self.ring(es, "ptr", 4, [128, 4, 128], BF16, psum=True)
        xbs = {}

        def stage_a(blk):
            xt = xin.next()
            self.dma("sp", xt.t[:], x_src[blk * 128:(blk + 1) * 128, :], [xsb[blk]], [xt.b], xt.sem)
            s = st.next()
            P.op("act", lambda e, xt=xt, s=s: e.activation(
                out=junk.t[:], in_=xt.t[:], func=AF.Square, accum_out=s.t[:, 0:1]),
                reads=[xt.b], writes=[junk.b, s.b])
            P.op("act", lambda e, s=s: e.activation(
                out=s.t[:, 1:2], in_=s.t[:, 0:1], func=AF.Ln, scale=1.0 / D, bias=EPS),
                reads=[s.b], writes=[s.b])
            P.op("act", lambda e, s=s: e.activation(
                out=s.t[:, 2:3], in_=s.t[:, 1:2], func=AF.Exp, scale=-0.5),
                reads=[s.b], writes=[s.b])
            xb = xn.next()
            P.op("dve", lambda e, xb=xb, xt=xt, s=s: e.tensor_scalar(
                out=xb.t[:], in0=xt.t[:], scalar1=s.t[:, 2:3], scalar2=None, op0=ALU.mult),
                reads=[xt.b, s.b], writes=[xb.b])
            xbs[blk] = xb

        def stage_b(blk):
            xb = xbs.pop(blk)
            for g in range(4):
                pt = ptr.next()
                for j in range(4):
                    kc = g * 4 + j
                    P.op("pe", lambda e, pt=pt, j=j, kc=kc, xb=xb: e.transpose(
                        pt.t[:, j, :], xb.t[:, kc * 128:(kc + 1) * 128], self.ident.t[:]),
                        reads=[xb.b, self.ident.b], writes=[pt.b], sig=(j == 3))
                for j in range(4):
                    kc = g * 4 + j
                    dst = hT.t[:, kc, blk * 128:(blk + 1) * 128]
                    if j % 2 == 0:
                        P.op("act", lambda e, pt=pt, j=j, kc=kc, dst=dst: e.activation(
                            out=dst, in_=pt.t[:, j, :], func=AF.Identity,
                            scale=self.Acol.t[:, site, kc:kc + 1], bias=self.Bcol.t[:, site, kc:kc + 1]),
                            reads=[pt.b, self.Acol.b, self.Bcol.b], writes=[hT.b])
                    else:
                        P.op("dve", lambda e, pt=pt, j=j, kc=kc, dst=dst: e.tensor_scalar(
                            out=dst, in0=pt.t[:, j, :], scalar1=self.Acol.t[:, site, kc:kc + 1],
                            scalar2=self.Bcol.t[:, site, kc:kc + 1], op0=ALU.mult, op1=ALU.add),
                            reads=[pt.b, self.Acol.b, self.Bcol.b], writes=[hT.b])

        stage_a(0)
        for blk in range(NBLK):
            if blk + 1 < NBLK:
                stage_a(blk + 1)
            stage_b(blk)

    def outproj_rings(self, es, nch):
        return (self.ring(es, "wo", 2, [128, nch, 256], BF16, dma=True),
                self.ring(es, "xc", 7, [128, 256], F32, dma=True),
                self.ring(es, "tm", 2, [128, 256], F32),
                self.ring(es, "py", 2, [128, 256], F32, psum=True))

    def outproj(self, es, actT, nch, tok0, nblk, w_rows, gate_bc, gscale, x_src, xsb_src, first, rings=None):
        nc, P = self.nc, self.P
        wo, xc, tm, py = rings if rings is not None else self.outproj_rings(es, nch)
        src, sbufs = (x_src, xsb_src) if first else (self.xs, self.xs_b)
        chunks = [(dc, bi) for dc in range(8) for bi in range(nblk)]
        LA = 5
        xts = {}

        def issue_load(i):
            dc, bi = chunks[i]
            blk = tok0 // 128 + bi
            xt = xc.next()
            xts[i] = xt
            self.dma("sp", xt.t[:], src[blk * 128:(blk + 1) * 128, dc * 256:(dc + 1) * 256],
                     [sbufs[blk]], [xt.b], xt.sem)

        for i in range(min(LA, len(chunks))):
            issue_load(i)
        wt = None
        for i, (dc, bi) in enumerate(chunks):
            blk = tok0 // 128 + bi
            if bi == 0:
                wt = wo.next()
                self.dma("pool", wt.t[:, 0:nch, :], w_rows(dc).rearrange("(c p) n -> p c n", p=128), [], [wt.b], wt.sem)
            if i + LA < len(chunks):
                issue_load(i + LA)
            xt = xts.pop(i)
            pt = py.next()
            for c in range(nch):
                P.op("pe", lambda e, pt=pt, c=c, bi=bi, wt=wt: e.matmul(
                    pt.t[:], actT.t[:, c, bi * 128:(bi + 1) * 128], wt.t[:, c, :],
                    start=(c == 0), stop=(c == nch - 1)),
                    reads=[actT.b, wt.b], writes=[pt.b], sig=(c == nch - 1))
            t = tm.next()
            P.op("dve", lambda e, t=t, pt=pt, dc=dc: e.scalar_tensor_tensor(
                out=t.t[:], in0=pt.t[:], scalar=gscale, in1=gate_bc.t[:, dc * 256:(dc + 1) * 256],
                op0=ALU.mult, op1=ALU.mult),
                reads=[pt.b, gate_bc.b], writes=[t.b])
            P.op("dve", lambda e, t=t, xt=xt: e.tensor_tensor(
                out=xt.t[:], in0=xt.t[:], in1=t.t[:], op=ALU.add),
                reads=[t.b, xt.b], writes=[xt.b])
            self.dma("sp", self.xs[blk * 128:(blk + 1) * 128, dc * 256:(dc + 1) * 256], xt.t[:],
                     [xt.b], [self.xs_b[blk]], xt.sem)

    def ffn(self, l, j, site, grow, x_src, xsb):
        nc, P = self.nc, self.P
        P.phase = f'ffn{l}{j}'
        w_in = self.ffn_w_in[l, j].rearrange("(kc p) n -> p kc n", p=128)
        w_out = self.ffn_w_out[l, j]
        with contextlib.ExitStack() as es:
            hT = Slot(self.sb(es, "hT", [128, KC, TOK], BF16))
            gate = Slot(self.sb(es, "gate", [128, D]), P.dsem())
            self.dma("sp", gate.t[:], self.gates[grow:grow + 1, :].partition_broadcast(128),
                     [self.gates_b], [gate.b], gate.sem)
            hid = Slot(self.sb(es, "hid", [128, 12, TOK], BF16))
            wg = self.ring(es, "wg", 2, [128, KC, 256], BF16, dma=True)
            wu = self.ring(es, "wu", 2, [128, KC, 256], BF16, dma=True)
            pg = self.ring(es, "pg", 2, [128, 512], F32, psum=True)
            pu = self.ring(es, "pu", 2, [128, 512], F32, psum=True)
            sg = self.ring(es, "sg", 2, [128, 512], F32)
            orings = self.outproj_rings(es, 12)
            class _V:
                pass
            pviews = []
            for sl in pg.slots + pu.slots:
                v_ = _V()
                v_.t = sl.t[:].bitcast(BF16).rearrange("p (a b) -> p a b", b=128)
                v_.b = sl.b
                v_.sem = None
                pviews.append(v_)
            with contextlib.ExitStack() as es2:
                self.prologue(es2, site, x_src, xsb, hT, ptr_slots=pviews, nxin=2)
            groups = [(0, 6), (6, 5), (11, 6), (17, 5)]
            for gi, (t0, nt) in enumerate(groups):
                for ti in range(nt):
                    t = t0 + ti
                    g_t, u_t = wg.next(), wu.next()
                    self.dma("pool", g_t.t[:], w_in[:, :, t * 256:(t + 1) * 256], [], [g_t.b], g_t.sem)
                    self.dma("pool", u_t.t[:], w_in[:, :, DFF + t * 256:DFF + (t + 1) * 256], [], [u_t.b], u_t.sem)
                    for sub in range(2):
                        c = ti * 2 + sub
                        for tt in range(4):
                            pgt, put = pg.next(), pu.next()
                            for (wt, pt) in ((g_t, pgt), (u_t, put)):
                                for kc in range(KC):
                                    P.op("pe", lambda e, wt=wt, pt=pt, kc=kc, sub=sub, tt=tt: e.matmul(
                                        pt.t[:], wt.t[:, kc, sub * 128:(sub + 1) * 128],
                                        hT.t[:, kc, tt * 512:(tt + 1) * 512],
                                        start=(kc == 0), stop=(kc == KC - 1)),
                                        reads=[wt.b, hT.b], writes=[pt.b], sig=(kc == KC - 1))
                            s = sg.next()
                            P.op("act", lambda e, s=s, pgt=pgt: e.activation(
                                out=s.t[:], in_=pgt.t[:], func=AF.Silu), reads=[pgt.b], writes=[s.b])
                            P.op("dve", lambda e, s=s, put=put, c=c, tt=tt: e.tensor_tensor(
                                out=hid.t[:, c, tt * 512:(tt + 1) * 512], in0=s.t[:], in1=put.t[:],
                                op=ALU.mult), reads=[s.b, put.b], writes=[hid.b])
                nch = nt * 2
                r0 = t0 * 256
                self.outproj(es, hid, nch, 0, NBLK,
                             lambda dc, r0=r0, nch=nch: w_out[r0:r0 + nch * 128, dc * 256:(dc + 1) * 256],
                             gate, 0.5, x_src, xsb, first=(gi == 0), rings=orings)
            P.barrier()
            P.emit()


    def trig(self, es, npart, invcol, phases, names):
        P = self.P
        tabs = [self.slot(es, nm, [npart, TOK]) for nm in names]
        invf = self.slot(es, "invf", [128, 4], dma=True)
        self.dma("sp", invf.t[:], self.c_invf, [], [invf.b], invf.sem)
        TWO_PI = 2.0 * PI
        with contextlib.ExitStack() as es2:
            posi = self.slot(es2, "posi", [npart, TOK], I32, dma=True)
            posf = self.slot(es2, "posf", [npart, TOK])
            u = self.slot(es2, "tu", [npart, TOK])
            ki = self.slot(es2, "tki", [npart, TOK], I32)
            kf = self.slot(es2, "tkf", [npart, TOK])
            m = self.slot(es2, "tm_", [npart, TOK])
            self.dma("sp", posi.t[:], self.pos.partition_broadcast(npart), [], [posi.b], posi.sem)
            P.op("dve", lambda e: e.tensor_copy(out=posf.t[:], in_=posi.t[:]), reads=[posi.b], writes=[posf.b])
            for tab, ph in zip(tabs, phases):
                P.op("dve", lambda e, ph=ph: e.tensor_scalar(
                    out=u.t[:], in0=posf.t[:], scalar1=invf.t[0:npart, invcol:invcol + 1],
                    scalar2=(ph if isinstance(ph, float) else invf.t[0:npart, ph:ph + 1]),
                    op0=ALU.mult, op1=ALU.add), reads=[posf.b, invf.b], writes=[u.b])
                P.op("dve", lambda e: e.tensor_scalar(
                    out=ki.t[:], in0=u.t[:], scalar1=1.0 / TWO_PI, scalar2=None, op0=ALU.mult),
                    reads=[u.b], writes=[ki.b])
                P.op("dve", lambda e: e.tensor_copy(out=kf.t[:], in_=ki.t[:]), reads=[ki.b], writes=[kf.b])
                P.op("dve", lambda e: e.scalar_tensor_tensor(
                    out=u.t[:], in0=kf.t[:], scalar=-TWO_PI, in1=u.t[:], op0=ALU.mult, op1=ALU.add),
                    reads=[kf.b, u.b], writes=[u.b])
                P.op("dve", lambda e: e.tensor_scalar(
                    out=m.t[:], in0=u.t[:], scalar1=PI, scalar2=TWO_PI, op0=ALU.is_gt, op1=ALU.mult),
                    reads=[u.b], writes=[m.b])
                P.op("dve", lambda e: e.tensor_tensor(out=u.t[:], in0=u.t[:], in1=m.t[:], op=ALU.subtract),
                     reads=[u.b, m.b], writes=[u.b])
                P.op("dve", lambda e: e.tensor_scalar(
                    out=m.t[:], in0=u.t[:], scalar1=-PI, scalar2=TWO_PI, op0=ALU.is_lt, op1=ALU.mult),
                    reads=[u.b], writes=[m.b])
                P.op("dve", lambda e: e.tensor_tensor(out=u.t[:], in0=u.t[:], in1=m.t[:], op=ALU.add),
                     reads=[u.b, m.b], writes=[u.b])
                P.op("dve", lambda e: e.tensor_scalar(
                    out=u.t[:], in0=u.t[:], scalar1=-3.141592, scalar2=3.141592, op0=ALU.max, op1=ALU.min),
                    reads=[u.b], writes=[u.b])
                P.op("act", lambda e, tab=tab: e.activation(out=tab.t[:], in_=u.t[:], func=AF.Sin),
                     reads=[u.b], writes=[tab.b])
            P.barrier()
            P.emit()
        return tabs

    def rope_fm(self, p0, p1, cos, sin, tsl, out0, out1, outb, rt):
        P = self.P
        t1, t2, t3, t4 = rt.next(), rt.next(), rt.next(), rt.next()
        for (t, p, tab) in ((t1, p0, cos), (t2, p1, sin), (t3, p0, sin), (t4, p1, cos)):
            P.op("dve", lambda e, t=t, p=p, tab=tab: e.tensor_tensor(
                out=t.t[:], in0=p.t[:], in1=tab.t[:, tsl], op=ALU.mult),
                reads=[p.b, tab.b], writes=[t.b])
        P.op("dve", lambda e: e.tensor_tensor(out=out0, in0=t1.t[:], in1=t2.t[:], op=ALU.subtract),
             reads=[t1.b, t2.b], writes=[outb])
        P.op("dve", lambda e: e.tensor_tensor(out=out1, in0=t3.t[:], in1=t4.t[:], op=ALU.add),
             reads=[t3.b, t4.b], writes=[outb])

    def retention(self):
        nc, P = self.nc, self.P
        P.phase = 'ret'
        w_in = self.ret_w_in[0].rearrange("(kc p) n -> p kc n", p=128)
        g128 = _consts()["_g128"]
        gsc = self.dscr("gsc", [TOK, 4096], BF16)
        gsc_b = [Buf() for _ in range(NBLK)]
        xsrc = [self.dscr(f"rxs{h}", [128, 1024]) for h in range(RET_H)]
        xdst = [self.dscr(f"rxd{h}", [256, 1024]) for h in range(RET_H)]
        csem = P.dsem()
        with contextlib.ExitStack() as es:
            hT = self.slot(es, "hT", [128, KC, TOK], BF16)
            with contextlib.ExitStack() as es2:
                self.prologue(es2, 1, self.xs, self.xs_b, hT)
                P.barrier()
                P.emit()
            sinT, cosT = self.trig(es, 128, 0, [0.0, PI / 2], ["sinT", "cosT"])
            qT = self.slot(es, "qT", [128, 2, TOK], BF16)
            qxT = self.slot(es, "qxT", [128, 2, TOK], BF16)
            kT = self.slot(es, "kT", [128, 2, TOK], BF16)
            kz = self.slot(es, "kz", [128, NBLK, 2, 128], BF16)
            v = self.slot(es, "v", [128, NBLK, 512], BF16)
            gg = self.slot(es, "gg", [128, NBLK, 512], BF16)
            wr = self.ring(es, "rw", 3, [128, KC, 256], BF16, dma=True)
            S_f = self.slot(es, "S_f", [128, 2, 512])
            S_br = self.ring(es, "S_b", 2, [128, 2, 512], BF16)
            Sst_sem = P.dsem()
            Sin_ = self.slot(es, "Sin", [128, 2, 512], F32, dma=True)
            gnb = self.slot(es, "gnb", [128, 512], F32, dma=True)
            dT = self.slot(es, "dT", [128, RET_H, 128], F32, dma=True)
            zeta = self.slot(es, "zeta", [128, RET_H], F32, dma=True)
            xi = self.slot(es, "xi", [128, RET_H, 128], F32, dma=True)
            rt = self.ring(es, "rrt", 4, [128, 512], F32)
            sTm = self.ring(es, "sTm", 4, [128, 128], BF16)
            onr = self.ring(es, "onr", 2, [128, 512], F32)
            stt = self.ring(es, "stt", 2, [128, 12], F32)
            gout = self.ring(es, "gout", 2, [128, 512], BF16, dma=True)
            pp = self.ring(es, "rpp", 6, [128, 512], F32, psum=True)
            ptr = self.ring(es, "rpt", 2, [128, 4, 128], BF16, psum=True)
            self.dma("sp", dT.t[:], self.c_dT, [], [dT.b], dT.sem)
            self.dma("sp", zeta.t[:], self.c_zeta, [], [zeta.b], zeta.sem)
            self.dma("sp", xi.t[:], self.c_xi, [], [xi.b], xi.sem)

            def proj_fm(wt, outT, h):
                for tt in range(4):
                    p0, p1 = pp.next(), pp.next()
                    for dch, p in ((0, p0), (1, p1)):
                        for kc in range(KC):
                            P.op("pe", lambda e, p=p, kc=kc, dch=dch, tt=tt: e.matmul(
                                p.t[:], wt.t[:, kc, dch * 128:(dch + 1) * 128],
                                hT.t[:, kc, tt * 512:(tt + 1) * 512],
                                start=(kc == 0), stop=(kc == KC - 1)),
                                reads=[wt.b, hT.b], writes=[p.b], sig=(kc == KC - 1))
                    tsl = slice(tt * 512, (tt + 1) * 512)
                    self.rope_fm(p0, p1, cosT, sinT, tsl, outT.t[:, 0, tsl], outT.t[:, 1, tsl], outT.b, rt)

            def proj_tm(w0, w1, evac):
                for blk in range(NBLK):
                    p = pp.next()
                    for half, wt in enumerate((w0, w1)):
                        for kc in range(KC):
                            P.op("pe", lambda e, p=p, kc=kc, half=half, wt=wt, blk=blk: e.matmul(
                                p.t[:, half * 256:(half + 1) * 256], hT.t[:, kc, blk * 128:(blk + 1) * 128],
                                wt.t[:, kc, :], start=(kc == 0), stop=(kc == KC - 1)),
                                reads=[wt.b, hT.b], writes=[p.b], sig=(kc == KC - 1))
                    evac(blk, p)

            def state_step(blk, h):
                pS = [pp.next(), pp.next()]
                for dch in range(2):
                    P.op("pe", lambda e, dch=dch, pS=pS, blk=blk: e.matmul(
                        pS[dch].t[:], kz.t[:, blk, dch, :], v.t[:, blk, :], start=True, stop=True),
                        reads=[kz.b, v.b], writes=[pS[dch].b])
                for dch in range(2):
                    P.op("dve", lambda e, dch=dch, pS=pS, h=h: e.scalar_tensor_tensor(
                        out=S_f.t[:, dch, :], in0=S_f.t[:, dch, :], scalar=g128[h], in1=pS[dch].t[:],
                        op0=ALU.mult, op1=ALU.add), reads=[S_f.b, pS[dch].b], writes=[S_f.b])

            for h in range(RET_H):
                wk = wr.next()
                self.dma("pool", wk.t[:], w_in[:, :, 2048 + h * 256:2048 + (h + 1) * 256], [], [wk.b], wk.sem)
                wv0, wv1 = wr.next(), wr.next()
                self.dma("pool", wv0.t[:], w_in[:, :, 4096 + h * 512:4096 + h * 512 + 256], [], [wv0.b], wv0.sem)
                self.dma("pool", wv1.t[:], w_in[:, :, 4096 + h * 512 + 256:4096 + (h + 1) * 512], [], [wv1.b], wv1.sem)
                proj_fm(wk, kT, h)
                for blk in range(NBLK):
                    pt = ptr.next()
                    for dch in range(2):
                        P.op("pe", lambda e, pt=pt, dch=dch, blk=blk: e.transpose(
                            pt.t[:, dch, :], kT.t[:, dch, blk * 128:(blk + 1) * 128], self.ident.t[:]),
                            reads=[kT.b, self.ident.b], writes=[pt.b], sig=(dch == 1))
                    P.op("act", lambda e, pt=pt, blk=blk, h=h: e.activation(
                        out=kz.t[:, blk, :, :], in_=pt.t[:, 0:2, :], func=AF.Copy, scale=zeta.t[:, h:h + 1]),
                        reads=[pt.b, zeta.b], writes=[kz.b])
                P.op("dve", lambda e: e.memset(S_f.t[:], 0.0), writes=[S_f.b])

                def evac_v(blk, p, h=h):
                    P.op("act", lambda e, blk=blk, p=p: e.copy(out=v.t[:, blk, :], in_=p.t[:]),
                         reads=[p.b], writes=[v.b])
                    state_step(blk, h)
                proj_tm(wv0, wv1, evac_v)
                xs_b_, xd_b_ = Buf(), Buf()
                self.dma("sp", xsrc[h].rearrange("p (a n) -> p a n", a=2), S_f.t[:], [S_f.b], [xs_b_], Sst_sem)
                P.op("pool", lambda e, h=h: e.collective_compute(
                    "AllGather", ALU.bypass, replica_groups=PAIRS, ins=[xsrc[h]], outs=[xdst[h]]),
                    reads=[xs_b_], writes=[xd_b_], dsem=csem, inc=None)
                self.dma("sp", Sin_.t[:], xdst[h][0:128, :].rearrange("p (a n) -> p a n", a=2),
                         [xd_b_], [Sin_.b], Sin_.sem)
                wq = wr.next()
                self.dma("pool", wq.t[:], w_in[:, :, h * 256:(h + 1) * 256], [], [wq.b], wq.sem)
                wg0, wg1 = wr.next(), wr.next()
                self.dma("pool", wg0.t[:], w_in[:, :, 8192 + h * 512:8192 + h * 512 + 256], [], [wg0.b], wg0.sem)
                self.dma("pool", wg1.t[:], w_in[:, :, 8192 + h * 512 + 256:8192 + (h + 1) * 512], [], [wg1.b], wg1.sem)
                self.dma("sp", gnb.t[:], self.ret_gn_g[0:1, h * 512:(h + 1) * 512].partition_broadcast(128),
                         [], [gnb.b], gnb.sem)
                proj_fm(wq, qT, h)
                for dch in range(2):
                    P.op("dve", lambda e, dch=dch, h=h: e.tensor_tensor(
                        out=qxT.t[:, dch, :].rearrange("p (b n) -> p b n", n=128),
                        in0=qT.t[:, dch, :].rearrange("p (b n) -> p b n", n=128),
                        in1=xi.t[:, h:h + 1, :].to_broadcast([128, NBLK, 128]), op=ALU.mult),
                        reads=[qT.b, xi.b], writes=[qxT.b])
                def evac_g(blk, p):
                    t = rt.next()
                    P.op("act", lambda e, t=t, p=p: e.activation(out=t.t[:], in_=p.t[:], func=AF.Silu),
                         reads=[p.b], writes=[t.b])
                    P.op("dve", lambda e, t=t, blk=blk: e.tensor_tensor(
                        out=gg.t[:, blk, :], in0=t.t[:], in1=gnb.t[:], op=ALU.mult),
                        reads=[t.b, gnb.b], writes=[gg.b])
                proj_tm(wg0, wg1, evac_g)
                P.op("dve", lambda e: e.tensor_scalar(
                    out=S_f.t[:], in0=Sin_.t[:], scalar1=self.flag_sb.t[:, 0:1], scalar2=None, op0=ALU.mult),
                    reads=[Sin_.b, self.flag_sb.b], writes=[S_f.b])
                S_b = S_br.next()
                P.op("act", lambda e, S_b=S_b: e.copy(out=S_b.t[:], in_=S_f.t[:]), reads=[S_f.b], writes=[S_b.b])

                def scores(blk, h=h):
                    bs = slice(blk * 128, (blk + 1) * 128)
                    ps = pp.next()
                    for dch in range(2):
                        P.op("pe", lambda e, ps=ps, dch=dch, bs=bs: e.matmul(
                            ps.t[:, 0:128], kT.t[:, dch, bs], qT.t[:, dch, bs],
                            start=(dch == 0), stop=(dch == 1)),
                            reads=[kT.b, qT.b], writes=[ps.b], sig=(dch == 1))
                    sm = sTm.next()
                    P.op("dve", lambda e, sm=sm, ps=ps, h=h: e.tensor_tensor(
                        out=sm.t[:], in0=ps.t[:, 0:128], in1=dT.t[:, h, :], op=ALU.mult),
                        reads=[ps.b, dT.b], writes=[sm.b])
                    return sm

                smq = [scores(0), scores(1)]
                for blk in range(NBLK):
                    bs = slice(blk * 128, (blk + 1) * 128)
                    sm = smq.pop(0)
                    if blk + 2 < NBLK:
                        smq.append(scores(blk + 2))
                    po = pp.next()
                    P.op("pe", lambda e, po=po, sm=sm, blk=blk: e.matmul(
                        po.t[:], sm.t[:], v.t[:, blk, :], start=True, stop=False),
                        reads=[sm.b, v.b], writes=[po.b], sig=False)
                    for dch in range(2):
                        P.op("pe", lambda e, po=po, dch=dch, bs=bs, S_b=S_b: e.matmul(
                            po.t[:], qxT.t[:, dch, bs], S_b.t[:, dch, :], start=False, stop=(dch == 1)),
                            reads=[qxT.b, S_b.b], writes=[po.b], sig=(dch == 1))
                    if blk < NBLK - 1:
                        state_step(blk, h)
                        S_b = S_br.next()
                        P.op("act", lambda e, S_b=S_b: e.copy(out=S_b.t[:], in_=S_f.t[:]),
                             reads=[S_f.b], writes=[S_b.b])
                    st = stt.next()
                    P.op("dve", lambda e, st=st, po=po: e.bn_stats(out=st.t[:, 0:6], in_=po.t[:]),
                         reads=[po.b], writes=[st.b])
                    P.op("dve", lambda e, st=st: e.bn_aggr(out=st.t[:, 6:8], in_=st.t[:, 0:6]),
                         reads=[st.b], writes=[st.b])
                    P.op("act", lambda e, st=st: e.activation(
                        out=st.t[:, 8:9], in_=st.t[:, 7:8], func=AF.Ln, bias=EPS), reads=[st.b], writes=[st.b])
                    P.op("act", lambda e, st=st: e.activation(
                        out=st.t[:, 9:10], in_=st.t[:, 8:9], func=AF.Exp, scale=-0.5), reads=[st.b], writes=[st.b])
                    on = onr.next()
                    P.op("dve", lambda e, on=on, po=po, st=st: e.tensor_scalar(
                        out=on.t[:], in0=po.t[:], scalar1=st.t[:, 6:7], scalar2=st.t[:, 9:10],
                        op0=ALU.subtract, op1=ALU.mult), reads=[po.b, st.b], writes=[on.b])
                    go = gout.next()
                    P.op("dve", lambda e, go=go, on=on, blk=blk: e.tensor_tensor(
                        out=go.t[:], in0=on.t[:], in1=gg.t[:, blk, :], op=ALU.mult),
                        reads=[on.b, gg.b], writes=[go.b])
                    self.dma("sp", gsc[bs, h * 512:(h + 1) * 512], go.t[:], [go.b], [gsc_b[blk]], go.sem)
            P.barrier()
            P.emit()
        P.release([csem, Sst_sem])
        with contextlib.ExitStack() as es:
            gate = self.slot(es, "gate", [128, D], F32, dma=True)
            self.dma("sp", gate.t[:], self.gates[1:2, :].partition_broadcast(128), [self.gates_b], [gate.b], gate.sem)
            self.tm_outproj(es, gsc, gsc_b, 32, lambda dc: self.ret_w_out[0][:, dc * 256:(dc + 1) * 256], gate, 1.0)

    def tm_outproj(self, es, act_dram, act_b, nch, w_rows, gate, gscale):
        P = self.P
        for th in range(2):
            with contextlib.ExitStack() as es2:
                aT = self.slot(es2, "aT", [128, nch, 1024], BF16)
                ain = self.ring(es2, "ain", 2, [128, nch * 128], BF16, dma=True)
                ptr = self.ring(es2, "opt", 2, [128, 4, 128], BF16, psum=True)
                for bi in range(8):
                    blk = th * 8 + bi
                    a = ain.next()
                    self.dma("sp", a.t[:], act_dram[blk * 128:(blk + 1) * 128, :], [act_b[blk]], [a.b], a.sem)
                    for g in range(nch // 4):
                        pt = ptr.next()
                        for j in range(4):
                            c = g * 4 + j
                            P.op("pe", lambda e, pt=pt, j=j, c=c, a=a: e.transpose(
                                pt.t[:, j, :], a.t[:, c * 128:(c + 1) * 128], self.ident.t[:]),
                                reads=[a.b, self.ident.b], writes=[pt.b], sig=(j == 3))
                        dst = aT.t[:, g * 4:(g + 1) * 4, bi * 128:(bi + 1) * 128]
                        if g % 2 == 0:
                            P.op("act", lambda e, pt=pt, dst=dst: e.copy(out=dst, in_=pt.t[:]),
                                 reads=[pt.b], writes=[aT.b])
                        else:
                            P.op("dve", lambda e, pt=pt, dst=dst: e.tensor_copy(out=dst, in_=pt.t[:]),
                                 reads=[pt.b], writes=[aT.b])
                self.outproj(es2, aT, nch, th * 1024, 8, w_rows, gate, gscale, self.xs, self.xs_b, first=False)
                P.barrier()
                P.emit()


    def latent_proj(self, es, hT, wt, gcol, outT, pp, ptr):
        P = self.P
        st = self.ring(es, "lst", 3, [128, 4], F32)
        junk = self.slot(es, "ljunk", [128, 512], BF16)
        cn = self.ring(es, "lcn", 2, [128, 512], BF16)
        for blk in range(NBLK):
            p = pp.next()
            for kc in range(KC):
                P.op("pe", lambda e, p=p, kc=kc, blk=blk: e.matmul(
                    p.t[:], hT.t[:, kc, blk * 128:(blk + 1) * 128], wt.t[:, kc, :],
                    start=(kc == 0), stop=(kc == KC - 1)),
                    reads=[hT.b, wt.b], writes=[p.b], sig=(kc == KC - 1))
            s = st.next()
            P.op("act", lambda e, p=p, s=s: e.activation(
                out=junk.t[:], in_=p.t[:], func=AF.Square, accum_out=s.t[:, 0:1]),
                reads=[p.b], writes=[junk.b, s.b])
            P.op("act", lambda e, s=s: e.activation(
                out=s.t[:, 1:2], in_=s.t[:, 0:1], func=AF.Ln, scale=1.0 / 512, bias=EPS),
                reads=[s.b], writes=[s.b])
            P.op("act", lambda e, s=s: e.activation(
                out=s.t[:, 2:3], in_=s.t[:, 1:2], func=AF.Exp, scale=-0.5), reads=[s.b], writes=[s.b])
            c_ = cn.next()
            P.op("dve", lambda e, c_=c_, p=p, s=s: e.tensor_scalar(
                out=c_.t[:], in0=p.t[:], scalar1=s.t[:, 2:3], scalar2=None, op0=ALU.mult),
                reads=[p.b, s.b], writes=[c_.b])
            pt = ptr.next()
            for j in range(4):
                P.op("pe", lambda e, pt=pt, j=j, c_=c_: e.transpose(
                    pt.t[:, j, :], c_.t[:, j * 128:(j + 1) * 128], self.ident.t[:]),
                    reads=[c_.b, self.ident.b], writes=[pt.b], sig=(j == 3))
            for j in range(4):
                dst = outT.t[:, j, blk * 128:(blk + 1) * 128]
                if j % 2 == 0:
                    P.op("act", lambda e, pt=pt, j=j, dst=dst: e.activation(
                        out=dst, in_=pt.t[:, j, :], func=AF.Copy, scale=gcol.t[:, j:j + 1]),
                        reads=[pt.b, gcol.b], writes=[outT.b])
                else:
                    P.op("dve", lambda e, pt=pt, j=j, dst=dst: e.tensor_scalar(
                        out=dst, in0=pt.t[:, j, :], scalar1=gcol.t[:, j:j + 1], scalar2=None, op0=ALU.mult),
                        reads=[pt.b, gcol.b], writes=[outT.b])

    def rope64(self, p1, p2, CS1, CS2, tsl, out, outb, rt):
        P = self.P
        t1, t2 = rt.next(), rt.next()
        P.op("dve", lambda e: e.tensor_tensor(out=t1.t[0:64, :], in0=p1.t[0:64, :], in1=CS1.t[:, tsl], op=ALU.mult),
             reads=[p1.b, CS1.b], writes=[t1.b])
        P.op("dve", lambda e: e.tensor_tensor(out=t2.t[0:64, :], in0=p2.t[0:64, :], in1=CS2.t[:, tsl], op=ALU.mult),
             reads=[p2.b, CS2.b], writes=[t2.b])
        P.op("dve", lambda e: e.tensor_tensor(out=out, in0=t1.t[0:64, :], in1=t2.t[0:64, :], op=ALU.add),
             reads=[t1.b, t2.b], writes=[outb])

    def shared_kv(self):
        nc, P = self.nc, self.P
        P.phase = 'kv'
        wdkv = self.w_dkv.rearrange("(kc p) n -> p kc n", p=128)
        self.ck_src = [self.dscr(f"cks{a}", [64, 4 * TOK], BF16) for a in range(2)]
        self.ck_dst = [self.dscr(f"ckd{a}", [128, 4 * TOK], BF16) for a in range(2)]
        self.kr_src = self.dscr("krs", [64, TOK], BF16)
        self.kr_dst = self.dscr("krd", [128, TOK], BF16)
        self.ck_src_b = [Buf(), Buf()]
        self.ck_dst_b = [Buf(), Buf()]
        self.kr_src_b, self.kr_dst_b = Buf(), Buf()
        csem = P.dsem()
        with contextlib.ExitStack() as es:
            hT = self.slot(es, "hT", [128, KC, TOK], BF16)
            with contextlib.ExitStack() as es2:
                self.prologue(es2, 3, self.xs, self.xs_b, hT)
                P.barrier()
                P.emit()
            CS1, CS2 = self.trig(es, 64, 1, [2, 3], ["CS1", "CS2"])
            wd = self.slot(es, "wd", [128, KC, 512], BF16, dma=True)
            wk1 = self.slot(es, "wk1", [128, KC, 64], BF16, dma=True)
            wk2 = self.slot(es, "wk2", [128, KC, 64], BF16, dma=True)
            gcol = self.slot(es, "kvg", [128, 4], F32, dma=True)
            ckvT = self.slot(es, "ckvT", [128, 4, TOK], BF16, dma=True)
            krT = self.slot(es, "krT", [64, TOK], BF16, dma=True)
            rt = self.ring(es, "krt", 4, [128, 512], F32)
            pp = self.ring(es, "kpp", 4, [128, 512], F32, psum=True)
            ptr = self.ring(es, "kpt", 2, [128, 4, 128], BF16, psum=True)
            self.dma("pool", wd.t[:], wdkv[:, :, 0:512], [], [wd.b], wd.sem)
            for (w, c0) in ((wk1, 512), (wk2, 544)):
                for d in range(2):
                    self.dma("pool", w.t[:, :, d * 32:(d + 1) * 32], wdkv[:, :, c0:c0 + 32], [], [w.b], w.sem)
            self.dma("sp", gcol.t[:], self.kvlat_col, [], [gcol.b], gcol.sem)
            self.latent_proj(es, hT, wd, gcol, ckvT, pp, ptr)
            for tt in range(4):
                tsl = slice(tt * 512, (tt + 1) * 512)
                p1, p2 = pp.next(), pp.next()
                for (w, p) in ((wk1, p1), (wk2, p2)):
                    for kc in range(KC):
                        P.op("pe", lambda e, p=p, w=w, kc=kc, tsl=tsl: e.matmul(
                            p.t[0:64, :], w.t[:, kc, :], hT.t[:, kc, tsl], start=(kc == 0), stop=(kc == KC - 1)),
                            reads=[w.b, hT.b], writes=[p.b], sig=(kc == KC - 1))
                self.rope64(p1, p2, CS1, CS2, tsl, krT.t[:, tsl], krT.b, rt)
            st_sems = [ckvT.sem, P.dsem()]
            es.callback(P.release, [st_sems[1]])
            for a in range(2):
                self.dma("sp", self.ck_src[a].rearrange("p (c n) -> p c n", c=4), ckvT.t[a * 64:(a + 1) * 64, :, :],
                         [ckvT.b], [self.ck_src_b[a]], st_sems[a])
            self.dma("sp", self.kr_src, krT.t[:], [krT.b], [self.kr_src_b], krT.sem)
            for (src, dst, sb_, db_) in ((self.ck_src[0], self.ck_dst[0], self.ck_src_b[0], self.ck_dst_b[0]),
                                         (self.ck_src[1], self.ck_dst[1], self.ck_src_b[1], self.ck_dst_b[1]),
                                         (self.kr_src, self.kr_dst, self.kr_src_b, self.kr_dst_b)):
                P.op("pool", lambda e, src=src, dst=dst: e.collective_compute(
                    "AllGather", ALU.bypass, replica_groups=PAIRS, ins=[src], outs=[dst]),
                    reads=[sb_], writes=[db_], dsem=csem, inc=None)
            P.barrier()
            P.emit()
        P.release([csem])

    def mla(self):
        nc, P = self.nc, self.P
        P.phase = 'mla'
        SCALE = 192.0 ** -0.5
        ao = self.dscr("ao", [TOK, D], BF16)
        ao_b = [Buf() for _ in range(NBLK)]
        wukv = self.w_ukv.rearrange("(c p) n -> p c n", p=128)
        wuq = self.w_uq[0].rearrange("(c p) n -> p c n", p=128)
        with contextlib.ExitStack() as es:
            cqT = self.slot(es, "cqT", [128, 4, TOK], BF16)
            with contextlib.ExitStack() as esA:
                hT = self.slot(esA, "hT", [128, KC, TOK], BF16)
                with contextlib.ExitStack() as es2:
                    self.prologue(es2, 5, self.xs, self.xs_b, hT)
                    P.barrier()
                    P.emit()
                wd = self.slot(esA, "wdq", [128, KC, 512], BF16, dma=True)
                gcol = self.slot(esA, "qg", [128, 4], F32, dma=True)
                pp = self.ring(esA, "qpp", 3, [128, 512], F32, psum=True)
                ptr = self.ring(esA, "qpt", 2, [128, 4, 128], BF16, psum=True)
                self.dma("pool", wd.t[:], self.w_dq[0].rearrange("(kc p) n -> p kc n", p=128), [], [wd.b], wd.sem)
                self.dma("sp", gcol.t[:], self.qlat_col, [], [gcol.b], gcol.sem)
                self.latent_proj(esA, hT, wd, gcol, cqT, pp, ptr)
                P.barrier()
                P.emit()
            CS1, CS2 = self.trig(es, 64, 1, [2, 3], ["qCS1", "qCS2"])
            ckvA = self.slot(es, "ckvA", [128, 4, 2 * TOK], BF16, dma=True)
            krA = self.slot(es, "krA", [64, 2 * TOK], BF16, dma=True)
            for a in range(2):
                self.dma("sp", ckvA.t[a * 64:(a + 1) * 64, :, 0:TOK],
                         self.ck_dst[a][0:64, :].rearrange("p (c n) -> p c n", c=4),
                         [self.ck_dst_b[a]], [ckvA.b], ckvA.sem)
                self.dma("sp", ckvA.t[a * 64:(a + 1) * 64, :, TOK:2 * TOK],
                         self.ck_src[a].rearrange("p (c n) -> p c n", c=4),
                         [self.ck_src_b[a]], [ckvA.b], ckvA.sem)
            self.dma("sp", krA.t[:, 0:TOK], self.kr_dst[0:64, :], [self.kr_dst_b], [krA.b], krA.sem)
            self.dma("sp", krA.t[:, TOK:2 * TOK], self.kr_src, [self.kr_src_b], [krA.b], krA.sem)
            msk = self.slot(es, "msk", [128, 4, 512], F32, dma=True)
            self.dma("sp", msk.t[:], self.c_mlamask, [], [msk.b], msk.sem)
            wk = self.ring(es, "wuk", 2, [128, 4, 128], BF16, dma=True)
            wv = self.ring(es, "wuv", 2, [128, 4, 128], BF16, dma=True)
            wqn = self.ring(es, "wqn", 2, [128, 4, 128], BF16, dma=True)
            wq1 = self.ring(es, "wq1", 2, [128, 4, 64], BF16, dma=True)
            wq2 = self.ring(es, "wq2", 2, [128, 4, 64], BF16, dma=True)
            knTr = self.ring(es, "knT", 2, [128, 2 * TOK], BF16)
            vhr = self.ring(es, "vh", 2, [128, 32, 130], BF16)
            qnTr = self.ring(es, "qnT", 2, [128, TOK], BF16)
            qrTr = self.ring(es, "qrT", 2, [64, TOK], BF16)
            PT = self.ring(es, "PT", 5, [128, 512], BF16)
            mtmp = self.ring(es, "mtmp", 3, [128, 512], F32)
            stt = self.ring(es, "mst", 4, [128, 4], F32)
            osb = self.ring(es, "osb", 4, [128, 128], BF16, dma=True)
            rt = self.ring(es, "mrt", 4, [128, 512], F32)
            pp = self.ring(es, "mpp", 4, [128, 512], F32, psum=True)
            por = self.ring(es, "mpo", 4, [128, 512], F32, psum=True)
            for sl in vhr.slots:
                P.op("dve", lambda e, sl=sl: e.memset(sl.t[:, :, 128:130], 1.0), writes=[sl.b])
            cnt = [0]

            def evac_copy(dst, src, reads, writes):
                cnt[0] += 1
                if cnt[0] % 2 == 0:
                    P.op("act", lambda e: e.copy(out=dst, in_=src), reads=reads, writes=writes)
                else:
                    P.op("dve", lambda e: e.tensor_copy(out=dst, in_=src), reads=reads, writes=writes)

            def head_proj(h):
                wk_h, wv_h, wqn_h, wq1_h, wq2_h = wk.next(), wv.next(), wqn.next(), wq1.next(), wq2.next()
                knT, vh, qnT, qrT = knTr.next(), vhr.next(), qnTr.next(), qrTr.next()
                self.dma("pool", wk_h.t[:], wukv[:, :, h * 256:h * 256 + 128], [], [wk_h.b], wk_h.sem)
                self.dma("pool", wv_h.t[:], wukv[:, :, h * 256 + 128:h * 256 + 256], [], [wv_h.b], wv_h.sem)
                self.dma("pool", wqn_h.t[:], wuq[:, :, h * 192:h * 192 + 128], [], [wqn_h.b], wqn_h.sem)
                for (w, c0) in ((wq1_h, h * 192 + 128), (wq2_h, h * 192 + 160)):
                    for d in range(2):
                        self.dma("pool", w.t[:, :, d * 32:(d + 1) * 32], wuq[:, :, c0:c0 + 32], [], [w.b], w.sem)
                for tt in range(8):
                    p = pp.next()
                    for c in range(4):
                        P.op("pe", lambda e, p=p, c=c, tt=tt: e.matmul(
                            p.t[:], wk_h.t[:, c, :], ckvA.t[:, c, tt * 512:(tt + 1) * 512],
                            start=(c == 0), stop=(c == 3)),
                            reads=[wk_h.b, ckvA.b], writes=[p.b], sig=(c == 3))
                    evac_copy(knT.t[:, tt * 512:(tt + 1) * 512], p.t[:], [p.b], [knT.b])
                for b4 in range(8):
                    p = pp.next()
                    for j in range(4):
                        blk = b4 * 4 + j
                        for c in range(4):
                            P.op("pe", lambda e, p=p, c=c, j=j, blk=blk: e.matmul(
                                p.t[:, j * 128:(j + 1) * 128], ckvA.t[:, c, blk * 128:(blk + 1) * 128],
                                wv_h.t[:, c, :], start=(c == 0), stop=(c == 3)),
                                reads=[wv_h.b, ckvA.b], writes=[p.b], sig=(c == 3 and j == 3))
                    evac_copy(vh.t[:, b4 * 4:(b4 + 1) * 4, 0:128], p.t[:].rearrange("p (a n) -> p a n", a=4),
                              [p.b], [vh.b])
                for tt in range(4):
                    tsl = slice(tt * 512, (tt + 1) * 512)
                    p = pp.next()
                    for c in range(4):
                        P.op("pe", lambda e, p=p, c=c, tsl=tsl: e.matmul(
                            p.t[:], wqn_h.t[:, c, :], cqT.t[:, c, tsl], start=(c == 0), stop=(c == 3)),
                            reads=[wqn_h.b, cqT.b], writes=[p.b], sig=(c == 3))
                    evac_copy(qnT.t[:, tsl], p.t[:], [p.b], [qnT.b])
                    p1, p2 = pp.next(), pp.next()
                    for (w, p_) in ((wq1_h, p1), (wq2_h, p2)):
                        for c in range(4):
                            P.op("pe", lambda e, p_=p_, w=w, c=c, tsl=tsl: e.matmul(
                                p_.t[0:64, :], w.t[:, c, :], cqT.t[:, c, tsl], start=(c == 0), stop=(c == 3)),
                                reads=[w.b, cqT.b], writes=[p_.b], sig=(c == 3))
                    self.rope64(p1, p2, CS1, CS2, tsl, qrT.t[:, tsl], qrT.b, rt)
                return knT, vh, qnT, qrT

            def attention(h, knT, vh, qnT, qrT):
                for g in range(4):
                    gsl = slice(g * 512, (g + 1) * 512)
                    kbs = list(range(16)) + [16 + j for j in range(4 * g + 4)]
                    pos_ = [por.next() for _ in range(4)]
                    pts = {}

                    def S(i):
                        kb = kbs[i]
                        ks = slice(kb * 128, (kb + 1) * 128)
                        p = pp.next()
                        P.op("pe", lambda e, p=p, ks=ks, gsl=gsl: e.matmul(
                            p.t[:], knT.t[:, ks], qnT.t[:, gsl], start=True, stop=False),
                            reads=[knT.b, qnT.b], writes=[p.b], sig=False)
                        P.op("pe", lambda e, p=p, ks=ks, gsl=gsl: e.matmul(
                            p.t[:], krA.t[:, ks], qrT.t[:, gsl], start=False, stop=True),
                            reads=[krA.b, qrT.b], writes=[p.b])
                        pt = PT.next()
                        pts[i] = pt
                        if kb < 16:
                            P.op("act", lambda e, p=p, pt=pt: e.activation(
                                out=pt.t[:], in_=p.t[:], func=AF.Exp, scale=SCALE, bias=self.flag_sb.t[:, 1:2]),
                                reads=[p.b, self.flag_sb.b], writes=[pt.b])
                        elif kb - 16 < 4 * g:
                            P.op("act", lambda e, p=p, pt=pt: e.activation(
                                out=pt.t[:], in_=p.t[:], func=AF.Exp, scale=SCALE),
                                reads=[p.b], writes=[pt.b])
                        else:
                            jj = kb - 16 - 4 * g
                            m_ = mtmp.next()
                            P.op("dve", lambda e, p=p, m_=m_, jj=jj: e.scalar_tensor_tensor(
                                out=m_.t[:], in0=p.t[:], scalar=SCALE, in1=msk.t[:, jj, :],
                                op0=ALU.mult, op1=ALU.add), reads=[p.b, msk.b], writes=[m_.b])
                            P.op("act", lambda e, m_=m_, pt=pt: e.activation(
                                out=pt.t[:], in_=m_.t[:], func=AF.Exp), reads=[m_.b], writes=[pt.b])

                    def PV(i):
                        kb = kbs[i]
                        pt = pts.pop(i)
                        for t in range(4):
                            P.op("pe", lambda e, t=t, pt=pt, kb=kb, i=i, po_=pos_[t], last=(i == len(kbs) - 1): e.matmul(
                                po_.t[:, 0:130], pt.t[:, t * 128:(t + 1) * 128], vh.t[:, kb, :],
                                start=(i == 0), stop=last),
                                reads=[pt.b, vh.b], writes=[pos_[t].b], sig=(i == len(kbs) - 1 or t == 3))

                    LAG = 3
                    for i in range(len(kbs) + LAG):
                        if i < len(kbs):
                            S(i)
                        if i >= LAG:
                            PV(i - LAG)
                    for t in range(4):
                        st = stt.next()
                        P.op("dve", lambda e, st=st, t=t: e.reciprocal(out=st.t[:, 0:1], in_=pos_[t].t[:, 128:129]),
                             reads=[pos_[t].b], writes=[st.b])
                        o_ = osb.next()
                        P.op("act", lambda e, o_=o_, st=st, t=t: e.activation(
                            out=o_.t[:], in_=pos_[t].t[:, 0:128], func=AF.Copy, scale=st.t[:, 0:1]),
                            reads=[pos_[t].b, st.b], writes=[o_.b])
                        qi = g * 4 + t
                        self.dma("sp", ao[qi * 128:(qi + 1) * 128, h * 128:(h + 1) * 128], o_.t[:],
                                 [o_.b], [ao_b[qi]], o_.sem)

            nxt = head_proj(0)
            for h in range(MLA_H):
                cur = nxt
                if h + 1 < MLA_H:
                    nxt = head_proj(h + 1)
                attention(h, *cur)
            P.barrier()
            P.emit()
        with contextlib.ExitStack() as es:
            gate = self.slot(es, "gate", [128, D], F32, dma=True)
            self.dma("sp", gate.t[:], self.gates[4:5, :].partition_broadcast(128), [self.gates_b], [gate.b], gate.sem)
            self.tm_outproj(es, ao, ao_b, 16, lambda dc: self.mla_w_out[0][:, dc * 256:(dc + 1) * 256], gate, 1.0)

    def final_out(self):
        nc, P = self.nc, self.P
        P.phase = 'final'
        src, sbufs = (self.x_in, self.xin_b) if self.stage < 1 else (self.xs, self.xs_b)
        with contextlib.ExitStack() as es:
            fg = Slot(self.sb(es, "fg", [128, D]), P.dsem())
            xin = self.ring(es, "fx", 4, [128, D], F32, dma=True)
            junk = Slot(self.sb(es, "fjunk", [128, D], BF16))
            st = self.ring(es, "fst", 4, [128, 4], F32)
            ob = [Buf() for _ in range(NBLK)]
            if self.stage >= 8:
                self.dma("sp", fg.t[:], self.final_g.partition_broadcast(128), [], [fg.b], fg.sem)
            xts = {}

            def load(blk):
                xt = xin.next()
                xts[blk] = xt
                self.dma("sp", xt.t[:], src[blk * 128:(blk + 1) * 128, :], [sbufs[blk]], [xt.b], xt.sem)

            for blk in range(min(3, NBLK)):
                load(blk)
            for blk in range(NBLK):
                xt = xts.pop(blk)
                if self.stage >= 8:
                    s = st.next()
                    P.op("act", lambda e, xt=xt, s=s: e.activation(
                        out=junk.t[:], in_=xt.t[:], func=AF.Square, accum_out=s.t[:, 0:1]),
                        reads=[xt.b], writes=[junk.b, s.b])
                    P.op("act", lambda e, s=s: e.activation(
                        out=s.t[:, 1:2], in_=s.t[:, 0:1], func=AF.Ln, scale=1.0 / D, bias=EPS),
                        reads=[s.b], writes=[s.b])
                    P.op("act", lambda e, s=s: e.activation(
                        out=s.t[:, 2:3], in_=s.t[:, 1:2], func=AF.Exp, scale=-0.5),
                        reads=[s.b], writes=[s.b])
                    P.op("dve", lambda e, xt=xt, s=s: e.scalar_tensor_tensor(
                        out=xt.t[:], in0=xt.t[:], scalar=s.t[:, 2:3], in1=fg.t[:], op0=ALU.mult, op1=ALU.mult),
                        reads=[xt.b, s.b, fg.b], writes=[xt.b])
                if blk + 3 < NBLK:
                    load(blk + 3)
                self.dma("sp", self.out[blk * 128:(blk + 1) * 128, :], xt.t[:], [xt.b], [ob[blk]], xt.sem)
            P.barrier()
            P.emit()


def _consts():
    bf = ml_dtypes.bfloat16
    c = {}
    c["c_ident"] = np.eye(128, dtype=np.float32).astype(bf)
    invf = np.zeros((128, 4), np.float32)
    invf[:, 0] = 10000.0 ** (-np.arange(128, dtype=np.float32) / 128)
    inv32 = 10000.0 ** (-np.arange(32, dtype=np.float32) / 32)
    invf[:64, 1] = np.concatenate([inv32, inv32])
    invf[:32, 2] = PI / 2
    invf[32:64, 2] = 0.0
    invf[:32, 3] = PI
    invf[32:64, 3] = PI / 2
    c["c_invf"] = invf
    log_g = np.log1p(-(2.0 ** (-5.0 - np.arange(RET_H, dtype=np.float32)))).astype(np.float32)
    n = np.arange(128, dtype=np.float32)
    chunk = (np.arange(128) // 64)
    dT = np.zeros((128, RET_H, 128), np.float32)
    for h in range(RET_H):
        d = np.exp(log_g[h] * np.abs(n[:, None] - n[None, :]))
        vis = (chunk[:, None] <= chunk[None, :])
        dT[:, h, :] = d * vis / 16.0
    c["c_dT"] = dT
    c["c_zeta"] = np.stack([np.exp(log_g[h] * (127.0 - n)) for h in range(RET_H)], 1).astype(np.float32)
    xi = (np.stack([np.exp(log_g[h] * (n + 1.0)) for h in range(RET_H)], 0) / 16.0).astype(np.float32)
    c["c_xi"] = np.broadcast_to(xi[None], (128, RET_H, 128)).copy()
    q_chunk = chunk[:, None]
    k_chunk = chunk[None, :]
    c["c_diag"] = np.where(k_chunk <= q_chunk, 0.0, -1e30).astype(np.float32)
    mm = np.zeros((128, 4, 512), np.float32)
    for jj in range(4):
        for t in range(4):
            if t < jj:
                mm[:, jj, t * 128:(t + 1) * 128] = -1e30
            elif t == jj:
                mm[:, jj, t * 128:(t + 1) * 128] = np.where(chunk[:, None] <= chunk[None, :], 0.0, -1e30)
    c["c_mlamask"] = mm
    c["_g128"] = [float(np.exp(log_g[h] * 128.0)) for h in range(RET_H)]
    return c


def _col(v):
    return np.ascontiguousarray(v.reshape(-1, 128).T)


def make_in_maps(inp, ncores=8):
    cst = _consts()
    shared = {
        "ffn_w_in": inp["ffn_w_in"],
        "ffn_w_out": inp["ffn_w_out"], "ret_w_in": inp["ret_w_in"], "ret_gn_g": inp["ret_gn_g"],
        "ret_w_out": inp["ret_w_out"], "mla_w_dkv": inp["mla_w_dkv"],
        "mla_w_ukv": inp["mla_w_ukv"], "mla_w_dq": inp["mla_w_dq"], "mla_w_uq": inp["mla_w_uq"],
        "mla_w_out": inp["mla_w_out"], "final_g": inp["final_g"].reshape(1, -1),
        "kvlat_col": _col(inp["kv_latent_g"]), "qlat_col": _col(inp["q_latent_g"][0]),
    }
    ng = inp["norm_g"]
    sites = [ng[0, 0], ng[0, 1], ng[0, 2], inp["kv_norm_g"], ng[1, 0], ng[1, 1], ng[1, 2]]
    shared["normg_col"] = np.ascontiguousarray(np.stack([_col(s) for s in sites], 1))
    for k, v in cst.items():
        if not k.startswith("_"):
            shared[k] = v
    wsrc = [(inp["ada_w"][0], inp["ada_b"][0]), (inp["kv_ada_w"], inp["kv_ada_b"]), (inp["ada_w"][1], inp["ada_b"][1])]
    bounds = np.cumsum([0] + [w.shape[1] for w, _ in wsrc])
    per = 20480

    def col_slice(c0, c1):
        ws, bs = [], []
        for k, (w, bv) in enumerate(wsrc):
            lo, hi = max(c0, bounds[k]), min(c1, bounds[k + 1])
            if lo < hi:
                ws.append(w[:, lo - bounds[k]:hi - bounds[k]])
                bs.append(bv[lo - bounds[k]:hi - bounds[k]])
        return np.ascontiguousarray(np.concatenate(ws, 1)), np.ascontiguousarray(np.concatenate(bs)[None, :])

    halves = [col_slice(0, per), col_slice(per, 2 * per)]
    maps = []
    for i in range(ncores):
        b, half = i // 2, i % 2
        m = dict(shared)
        m["mods_w"], m["mods_b"] = halves[half]
        m["c_col"] = _col(inp["c"][b])
        m["x_in"] = np.ascontiguousarray(inp["x"][b, half * TOK:(half + 1) * TOK, :])
        m["pos"] = np.ascontiguousarray(inp["positions"][b, half * TOK:(half + 1) * TOK].reshape(1, TOK)).astype(np.int32)
        fl = np.zeros((128, 2), np.float32)
        fl[:, 0] = float(half)
        fl[:, 1] = 0.0 if half == 1 else -1e30
        m["flag"] = fl
        maps.append(m)
    return maps


_NC_CACHE = {}


def kernel(**inputs):
    inp = {k: np.asarray(v) for k, v in inputs.items()}
    if "full" not in _NC_CACHE:
        _NC_CACHE["full"] = K(stage=99).build()
    nc = _NC_CACHE["full"]
    maps = make_in_maps(inp)
    res = run_bass_kernel_spmd(nc, maps, core_ids=list(range(8)))
    out = np.empty((4, SEQ, D), np.float32)
    for i in range(8):
        b, half = i // 2, i % 2
        out[b, half * TOK:(half + 1) * TOK, :] = np.asarray(res.results[i]["out"])
    return out
```
